# Optimizing a Trainium2 kernel written in Bass

```python
import math
import jax, jax.numpy as jnp
from jax import lax
import numpy as np


D_MODEL = 1024
BATCH = 8
SEQ = 4096
DEPTH = 2

DEEPNORM_ALPHA = (2 * DEPTH) ** 0.25
DEEPNORM_BETA = (8 * DEPTH) ** -0.25
LN_EPS = 1e-5
ROPE_THETA = 500000.0
ROPE_FRACTION = 4
RWKV_HEAD_DIM = 64
RWKV_DIM = D_MODEL // 2
RWKV_HEADS = RWKV_DIM // RWKV_HEAD_DIM
DECAY_LORA = max(32, int(round(1.8 * D_MODEL ** 0.5 / 32)) * 32)
ICLR_LORA = max(32, int(round(1.8 * D_MODEL ** 0.5 / 32)) * 32)
GATE_LORA = max(32, int(round(0.6 * D_MODEL ** 0.8 / 32)) * 32)
RWKV_GN_EPS = 64e-5
RWKV_PROJ_COLS = 3 * RWKV_DIM + DECAY_LORA + ICLR_LORA + GATE_LORA
MOBA_HEAD_DIM = 128
MOBA_WIDTH = D_MODEL - RWKV_DIM
MOBA_HEADS = MOBA_WIDTH // MOBA_HEAD_DIM
MOBA_BLOCK = 256
MOBA_TOPK = 3
MOBA_Q_CHUNK = 16
AB_IN_COLS = RWKV_PROJ_COLS + 3 * MOBA_WIDTH
AB_OUT_COLS = RWKV_DIM + MOBA_WIDTH
DIL_PAIRS = ((128, 1), (512, 4), (2048, 16))
DIL_GROUPS = len(DIL_PAIRS)
DIL_HEAD_DIM = 128
DIL_HEADS = D_MODEL // DIL_HEAD_DIM
C_WIDTH = DIL_HEADS * DIL_HEAD_DIM
C_IN_COLS = DIL_GROUPS * C_WIDTH + 2 * C_WIDTH
MLP_HIDDEN = 4 * D_MODEL
NEG_INF = -1e30

kernel_name = 'hybrid_rwkv7_moba_dilated_deepnorm'


def _layer_norm(x, g, b):
    xf = x.astype(jnp.float32)
    mu = jnp.mean(xf, -1, keepdims=True)
    var = jnp.mean(jnp.square(xf - mu), -1, keepdims=True)
    return ((xf - mu) * lax.rsqrt(var + LN_EPS) * g + b).astype(x.dtype)


def _partial_rotary(x, pos):
    rot = x.shape[-1] // ROPE_FRACTION
    half = rot // 2
    inv_freq = ROPE_THETA ** (-jnp.arange(half, dtype=jnp.float32) / half)
    ang = pos.astype(jnp.float32)[:, None] * inv_freq[None, :]
    cos = jnp.cos(ang).astype(x.dtype)
    sin = jnp.sin(ang).astype(x.dtype)
    x1, x2, rest = x[..., :half], x[..., half:rot], x[..., rot:]
    return jnp.concatenate([x1 * cos - x2 * sin, x2 * cos + x1 * sin, rest], axis=-1)


def _token_shift(z):
    return jnp.pad(z, ((0, 0), (1, 0), (0, 0)))[:, :-1]


def _rwkv7_time_mix(z, shift_mix, w0, w_up, a0, a_up, g_up, k_k, k_a, r_k, lnx_g, lnx_b):
    B, T, _ = z.shape
    H, N = RWKV_HEADS, RWKV_HEAD_DIM
    f32 = jnp.float32
    z = z + (_token_shift(z) - z) * shift_mix
    o1, o2, o3 = RWKV_DIM, 2 * RWKV_DIM, 3 * RWKV_DIM
    o4 = o3 + DECAY_LORA
    o5 = o4 + ICLR_LORA
    r, k, v = z[..., :o1], z[..., o1:o2], z[..., o2:o3]
    w_d, a_d, g_d = z[..., o3:o4], z[..., o4:o5], z[..., o5:]
    w = -jax.nn.softplus(-(w0 + jnp.tanh(w_d) @ w_up)) - 0.5
    decay = jnp.exp(-jnp.exp(w.astype(f32)))
    a = jax.nn.sigmoid(a0 + a_d @ a_up).astype(f32)
    g = jax.nn.sigmoid(g_d) @ g_up
    heads = lambda t: t.astype(f32).reshape(B, T, H, N)
    kk = heads(k * k_k)
    kk = kk / jnp.maximum(jnp.linalg.norm(kk, axis=-1, keepdims=True), 1e-12)
    k = k.astype(f32) * (1.0 + (a - 1.0) * k_a)
    r, k, v, a, decay = heads(r), heads(k), heads(v), heads(a), heads(decay)

    def step(S, inp):
        r_t, w_t, k_t, v_t, a_t, b_t = inp
        sa = jnp.einsum('bhvk,bhk->bhv', S, a_t)
        S = S * w_t[:, :, None, :] + sa[..., None] * b_t[:, :, None, :] + v_t[..., None] * k_t[:, :, None, :]
        return S, jnp.einsum('bhvk,bhk->bhv', S, r_t)

    seq_first = lambda t: jnp.moveaxis(t, 1, 0)
    xs = (seq_first(r), seq_first(decay), seq_first(k), seq_first(v), seq_first(-kk), seq_first(kk * a))
    _, y = lax.scan(step, jnp.zeros((B, H, N, N), f32), xs)
    y = jnp.moveaxis(y, 0, 1)
    mu = jnp.mean(y, -1, keepdims=True)
    var = jnp.mean(jnp.square(y - mu), -1, keepdims=True)
    y = ((y - mu) * lax.rsqrt(var + RWKV_GN_EPS)).reshape(B, T, RWKV_DIM) * lnx_g + lnx_b
    bonus = jnp.sum(r * k * r_k, -1, keepdims=True) * v
    y = y + bonus.reshape(B, T, RWKV_DIM)
    return y * g


def _moba_attention(q, k, v):
    B, H, T, Dh = q.shape
    f32 = jnp.float32
    nb = -(-T // MOBA_BLOCK)
    Tp = nb * MOBA_BLOCK
    pad = ((0, 0), (0, 0), (0, Tp - T), (0, 0))
    q, k, v = jnp.pad(q, pad), jnp.pad(k, pad), jnp.pad(v, pad)
    scale = Dh ** -0.5
    kb = k.reshape(B, H, nb, MOBA_BLOCK, Dh)
    vb = v.reshape(B, H, nb, MOBA_BLOCK, Dh)
    n_sel = min(MOBA_TOPK, nb - 1)
    sel = None
    if n_sel > 0:
        k_mean = jnp.mean(kb.astype(f32), axis=3)
        gate = jnp.einsum('bhtd,bhnd->bhtn', q.astype(f32), k_mean)
        q_blk = jnp.arange(Tp) // MOBA_BLOCK
        past = jnp.arange(nb)[None, :] < q_blk[:, None]
        gate = jnp.where(past, gate, NEG_INF)
        _, sel = lax.top_k(gate, n_sel)
    bi = jnp.arange(B)[:, None, None, None]
    hi = jnp.arange(H)[None, :, None, None]

    def chunk(c):
        t0 = c * MOBA_Q_CHUNK
        i = t0 // MOBA_BLOCK
        qc = lax.dynamic_slice_in_dim(q, t0, MOBA_Q_CHUNK, axis=2)
        k_own = lax.dynamic_slice_in_dim(k, i * MOBA_BLOCK, MOBA_BLOCK, axis=2)
        v_own = lax.dynamic_slice_in_dim(v, i * MOBA_BLOCK, MOBA_BLOCK, axis=2)
        s_own = jnp.einsum('bhqd,bhkd->bhqk', qc, k_own).astype(f32) * scale
        causal = (i * MOBA_BLOCK + jnp.arange(MOBA_BLOCK))[None, :] <= (t0 + jnp.arange(MOBA_Q_CHUNK))[:, None]
        s_own = jnp.where(causal, s_own, NEG_INF)
        if n_sel == 0:
            p = jax.nn.softmax(s_own, axis=-1).astype(v.dtype)
            return jnp.einsum('bhqk,bhkd->bhqd', p, v_own)
        idx = lax.dynamic_slice_in_dim(sel, t0, MOBA_Q_CHUNK, axis=2)
        k_sel = kb[bi, hi, idx]
        v_sel = vb[bi, hi, idx]
        s_sel = jnp.einsum('bhqd,bhqskd->bhqsk', qc, k_sel).astype(f32) * scale
        valid = jnp.arange(n_sel) < i
        s_sel = jnp.where(valid[:, None], s_sel, NEG_INF)
        n_k = n_sel * MOBA_BLOCK
        s_all = jnp.concatenate([s_sel.reshape(B, H, MOBA_Q_CHUNK, n_k), s_own], axis=-1)
        p = jax.nn.softmax(s_all, axis=-1).astype(v.dtype)
        p_sel = p[..., :n_k].reshape(B, H, MOBA_Q_CHUNK, n_sel, MOBA_BLOCK)
        return (jnp.einsum('bhqsk,bhqskd->bhqd', p_sel, v_sel)
                + jnp.einsum('bhqk,bhkd->bhqd', p[..., n_k:], v_own))

    out = lax.map(chunk, jnp.arange(Tp // MOBA_Q_CHUNK))
    out = jnp.moveaxis(out, 0, 2).reshape(B, H, Tp, Dh)
    return out[:, :, :T]


def _dilated_window_branch(q, k, v, span, dilation):
    B, H, T, Dh = q.shape
    f32 = jnp.float32
    L = T // dilation
    W = span // dilation
    nblk = -(-L // W)
    Lp = nblk * W

    def strided(t):
        t = t.reshape(B, H, L, dilation, Dh).transpose(0, 1, 3, 2, 4)
        return jnp.pad(t, ((0, 0), (0, 0), (0, 0), (0, Lp - L), (0, 0)))

    qb = strided(q).reshape(B, H, dilation, nblk, W, Dh)
    kb = strided(k).reshape(B, H, dilation, nblk, W, Dh)
    vb = strided(v).reshape(B, H, dilation, nblk, W, Dh)
    prev = lambda t: jnp.pad(t, ((0, 0), (0, 0), (0, 0), (1, 0), (0, 0), (0, 0)))[:, :, :, :-1]
    k_band = jnp.concatenate([prev(kb), kb], axis=4)
    v_band = jnp.concatenate([prev(vb), vb], axis=4)
    s = jnp.einsum('bhrnqd,bhrnkd->bhrnqk', qb, k_band).astype(f32) * (Dh ** -0.5)
    qi = jnp.arange(W)[:, None]
    kj = jnp.arange(2 * W)[None, :]
    dist = W + qi - kj
    blk = jnp.arange(nblk)[:, None, None]
    mask = (dist >= 0) & (dist <= W) & ((blk > 0) | (kj >= W))
    s = jnp.where(mask, s, NEG_INF)
    lse = jax.nn.logsumexp(s, axis=-1)
    p = jnp.exp(s - lse[..., None]).astype(v.dtype)
    o = jnp.einsum('bhrnqk,bhrnkd->bhrnqd', p, v_band)
    o = o.reshape(B, H, dilation, Lp, Dh)[:, :, :, :L].transpose(0, 1, 3, 2, 4).reshape(B, H, T, Dh)
    lse = lse.reshape(B, H, dilation, Lp)[..., :L].transpose(0, 1, 3, 2).reshape(B, H, T)
    return o, lse


def _rwkv_moba_mixer(x, w_in, shift_mix, w0, w_up, a0, a_up, g_up, k_k, k_a, r_k, lnx_g, lnx_b, w_out, pos):
    B, T, _ = x.shape
    z = x @ w_in
    y_a = _rwkv7_time_mix(z[..., :RWKV_PROJ_COLS], shift_mix, w0, w_up, a0, a_up, g_up,
                          k_k, k_a, r_k, lnx_g, lnx_b)
    qkv = z[..., RWKV_PROJ_COLS:].reshape(B, T, 3, MOBA_HEADS, MOBA_HEAD_DIM).transpose(2, 0, 3, 1, 4)
    q = _partial_rotary(qkv[0], pos)
    k = _partial_rotary(qkv[1], pos)
    y_b = _moba_attention(q, k, qkv[2]).transpose(0, 2, 1, 3).reshape(B, T, MOBA_WIDTH)
    y = jnp.concatenate([y_a.astype(x.dtype), y_b.astype(x.dtype)], axis=-1)
    return (y @ w_out).astype(x.dtype)


def _dilated_mixer(x, w_in, w_out, pos):
    B, T, _ = x.shape
    z = x @ w_in
    nq = DIL_GROUPS * C_WIDTH
    q = z[..., :nq].reshape(B, T, DIL_GROUPS, DIL_HEADS, DIL_HEAD_DIM).transpose(0, 2, 3, 1, 4)
    kv = z[..., nq:].reshape(B, T, 2, DIL_HEADS, DIL_HEAD_DIM).transpose(2, 0, 3, 1, 4)
    q = _partial_rotary(q, pos)
    k = _partial_rotary(kv[0], pos)
    v = kv[1]
    outs, lses = [], []
    for g, (span, dilation) in enumerate(DIL_PAIRS):
        o, l = _dilated_window_branch(q[:, g], k, v, span, dilation)
        outs.append(o)
        lses.append(l)
    wts = jax.nn.softmax(jnp.stack(lses, axis=0), axis=0).astype(v.dtype)
    o = jnp.einsum('gbht,gbhtd->bhtd', wts, jnp.stack(outs, axis=0))
    o = o.transpose(0, 2, 1, 3).reshape(B, T, C_WIDTH)
    return (o @ w_out).astype(x.dtype)


def _sq_relu_mlp(x, w1, w2):
    return jnp.square(jax.nn.relu(x @ w1)) @ w2


def setup_inputs(seed: int = 0) -> dict:
    key = jax.random.key(seed)
    ks = iter(jax.random.split(key, 24))
    f32 = jnp.float32
    nrm = lambda shape, scale: jax.random.normal(next(ks), shape, f32) * scale
    D = D_MODEL
    NE = (DEPTH + 1) // 2
    NO = DEPTH // 2
    x = nrm((BATCH, SEQ, D), 1.0)
    ab_w_in = nrm((NE, D, AB_IN_COLS), D ** -0.5)
    ab_shift_mix = jax.random.uniform(next(ks), (NE, RWKV_PROJ_COLS), f32)
    ab_w0 = jax.random.uniform(next(ks), (NE, RWKV_DIM), f32, minval=-6.0, maxval=0.0)
    ab_w_up = nrm((NE, DECAY_LORA, RWKV_DIM), 0.5 * DECAY_LORA ** -0.5)
    ab_a0 = nrm((NE, RWKV_DIM), 0.1)
    ab_a_up = nrm((NE, ICLR_LORA, RWKV_DIM), 0.5 * ICLR_LORA ** -0.5)
    ab_g_up = nrm((NE, GATE_LORA, RWKV_DIM), GATE_LORA ** -0.5)
    ab_k_k = 0.85 + nrm((NE, RWKV_DIM), 0.02)
    ab_k_a = 1.0 + nrm((NE, RWKV_DIM), 0.02)
    ab_r_k = nrm((NE, RWKV_HEADS, RWKV_HEAD_DIM), 0.1)
    ab_lnx_g = 1.0 + nrm((NE, RWKV_DIM), 0.02)
    ab_lnx_b = nrm((NE, RWKV_DIM), 0.02)
    ab_w_out = nrm((NE, AB_OUT_COLS, D), DEEPNORM_BETA * AB_OUT_COLS ** -0.5)
    c_w_in = nrm((NO, D, C_IN_COLS), D ** -0.5)
    c_w_out = nrm((NO, C_WIDTH, D), DEEPNORM_BETA * C_WIDTH ** -0.5)
    ln1_g = 1.0 + nrm((DEPTH, D), 0.02)
    ln1_b = nrm((DEPTH, D), 0.02)
    mlp_w1 = nrm((DEPTH, D, MLP_HIDDEN), D ** -0.5)
    mlp_w2 = nrm((DEPTH, MLP_HIDDEN, D), DEEPNORM_BETA * MLP_HIDDEN ** -0.5)
    ln2_g = 1.0 + nrm((DEPTH, D), 0.02)
    ln2_b = nrm((DEPTH, D), 0.02)
    return {'x': x, 'ab_w_in': ab_w_in, 'ab_shift_mix': ab_shift_mix, 'ab_w0': ab_w0,
            'ab_w_up': ab_w_up, 'ab_a0': ab_a0, 'ab_a_up': ab_a_up, 'ab_g_up': ab_g_up,
            'ab_k_k': ab_k_k, 'ab_k_a': ab_k_a, 'ab_r_k': ab_r_k, 'ab_lnx_g': ab_lnx_g,
            'ab_lnx_b': ab_lnx_b, 'ab_w_out': ab_w_out, 'c_w_in': c_w_in, 'c_w_out': c_w_out,
            'ln1_g': ln1_g, 'ln1_b': ln1_b, 'mlp_w1': mlp_w1, 'mlp_w2': mlp_w2,
            'ln2_g': ln2_g, 'ln2_b': ln2_b}


def reference(x, ab_w_in, ab_shift_mix, ab_w0, ab_w_up, ab_a0, ab_a_up, ab_g_up, ab_k_k, ab_k_a,
              ab_r_k, ab_lnx_g, ab_lnx_b, ab_w_out, c_w_in, c_w_out, ln1_g, ln1_b, mlp_w1, mlp_w2,
              ln2_g, ln2_b):
    T = x.shape[1]
    pos = jnp.arange(T)
    for layer in range(DEPTH):
        j = layer // 2
        if layer % 2 == 0:
            h = _rwkv_moba_mixer(x, ab_w_in[j], ab_shift_mix[j], ab_w0[j], ab_w_up[j], ab_a0[j],
                                 ab_a_up[j], ab_g_up[j], ab_k_k[j], ab_k_a[j], ab_r_k[j],
                                 ab_lnx_g[j], ab_lnx_b[j], ab_w_out[j], pos)
        else:
            h = _dilated_mixer(x, c_w_in[j], c_w_out[j], pos)
        x = _layer_norm(DEEPNORM_ALPHA * x + h, ln1_g[layer], ln1_b[layer])
        x = _layer_norm(DEEPNORM_ALPHA * x + _sq_relu_mlp(x, mlp_w1[layer], mlp_w2[layer]),
                        ln2_g[layer], ln2_b[layer])
    return x
```

```python
import numpy as np
from contextlib import ExitStack
import concourse.bass as bass
import concourse.mybir as mybir
from concourse.bass_utils import run_bass_kernel_spmd

F32 = mybir.dt.float32
BF16 = mybir.dt.bfloat16
AF = mybir.ActivationFunctionType
ALU = mybir.AluOpType
AX = mybir.AxisListType

ENGS = ("sync", "act", "dve", "pool", "pe")
DEFAULT_COST = {"sync": 0.05, "act": 0.5, "dve": 0.55, "pool": 0.7, "pe": 0.12, "dma": 3.0}
SEM_LAT = 0.4
SCHEDULE = True
CRITPATH = False


class _Op:
    __slots__ = ("eng", "fn", "deps", "odeps", "dma_key", "dma_cnt", "sig", "cnt", "cost", "idx", "succ", "nd", "rt", "tag", "ef", "cp")


class Prog:
    def __init__(self, nc):
        self.nc = nc
        self.es = ExitStack()
        self.engsem = {e: self.es.enter_context(nc.semaphore(f"s_{e}")) for e in ENGS}
        self.engcnt = {e: 0 for e in ENGS}
        self.dma_pool = []
        self.dma_free = {}
        self.nstage = 0

    def get_dma_sem(self, cls):
        fl = self.dma_free.setdefault(cls, [])
        if fl:
            return fl.pop()
        h = self.es.enter_context(self.nc.semaphore(f"s_dma{len(self.dma_pool)}"))
        ent = [h, 0, cls]
        self.dma_pool.append(ent)
        return ent

    def close(self):
        self.es.close()


class Stage:
    def __init__(self, prog, name):
        self.prog = prog
        self.nc = prog.nc
        self.name = name
        self.ops = {e: [] for e in ENGS}
        self.lastw = {}
        self.readers = {}
        self.dma_ents = {}
        self.dma_base = {}
        self.dma_counts = {}
        self.order = []
        self.last_dma = {}
        self.ns = None
        self.costs = {}
        self.es = ExitStack()

    def sb(self, name, shape, dt):
        return self.es.enter_context(self.nc.sbuf_tensor(f"{self.name}_{self.ns or ""}{name}", list(shape), dt))

    def ps(self, name, shape, dt):
        return self.es.enter_context(self.nc.psum_tensor(f"{self.name}_{self.ns or ""}{name}", list(shape), dt))

    def _mk(self, eng, fn, reads, writes, dma_key=None, cost=None):
        if self.ns is not None:
            reads = [(self.ns, r) for r in reads]
            writes = [(self.ns, w) for w in writes]
            if dma_key is not None:
                dma_key = (self.ns, dma_key)
        op = _Op()
        op.eng = eng
        op.fn = fn
        op.dma_key = dma_key
        op.sig = False
        op.cnt = None
        op.dma_cnt = None
        deps = {}
        for r in reads:
            w = self.lastw.get(r)
            if w is not None:
                deps[id(w)] = w
        for w_ in writes:
            w = self.lastw.get(w_)
            if w is not None:
                deps[id(w)] = w
            for rd in self.readers.get(w_, ()):
                deps[id(rd)] = rd
        dl = []
        ol = []
        for d in deps.values():
            if d is op:
                continue
            if d.dma_key is None and d.eng == "pe" and eng == "pe" and dma_key is None:
                ol.append(d)
                continue
            if d.dma_key is not None:
                dl.append((d, self.dma_ents[d.dma_key][1]))
                ld = self.last_dma.get(d.dma_key)
                if ld is not None and ld is not d:
                    ol.append(ld)
            else:
                dl.append((d, None))
                d.sig = True
        op.deps = dl
        op.odeps = ol
        op.cost = cost if cost is not None else (self.costs.get(eng, DEFAULT_COST[eng]) if dma_key is None else DEFAULT_COST["dma"])
        op.idx = len(self.order)
        import sys as _s
        op.tag = _s._getframe(2).f_lineno
        self.order.append(op)
        if dma_key is not None:
            prev = self.last_dma.get(dma_key)
            if prev is not None:
                ol.append(prev)
            self.last_dma[dma_key] = op
        rk = dma_key if dma_key is not None else eng
        for r in reads:
            self.readers.setdefault(r, []).append(op)
        for w_ in writes:
            self.lastw[w_] = op
            self.readers[w_] = []
        if dma_key is not None:
            if dma_key not in self.dma_ents:
                ent = self.prog.get_dma_sem("sw" if eng == "pool" else "hw")
                self.dma_ents[dma_key] = ent
            ent = self.dma_ents[dma_key]
            assert ent[2] == ("sw" if eng == "pool" else "hw"), dma_key
            ent[1] += 1
            op.dma_cnt = ent[1]
        self.ops[eng].append(op)
        return op

    def op(self, eng, fn, reads=(), writes=(), c=None):
        return self._mk(eng, fn, reads, writes, cost=c)

    def dma(self, eng, out, in_, reads=(), writes=(), key=None, **kw):
        assert key is not None
        return self._mk(eng, lambda e: e.dma_start(out=out, in_=in_, **kw), reads, writes, dma_key=key)

    def schedule(self):
        import heapq
        ops = self.order
        for op in ops:
            op.succ = []
            op.rt = 0.0
        for op in ops:
            ds = {id(d): (d, False) for d, _ in op.deps}
            for d in op.odeps:
                if id(d) not in ds:
                    ds[id(d)] = (d, True)
            op.nd = len(ds)
            for d, oo in ds.values():
                d.succ.append((op, oo))
        pend = {e: [] for e in ENGS}
        avail = {e: [] for e in ENGS}
        free = {e: 0.0 for e in ENGS}
        fin = {}
        for op in ops:
            if op.nd == 0:
                heapq.heappush(pend[op.eng], (0.0, op.idx, op))
        sched = {e: [] for e in ENGS}
        n = 0
        while n < len(ops):
            best = None
            for e in ENGS:
                pe_, av = pend[e], avail[e]
                while pe_ and pe_[0][0] <= free[e]:
                    _, i_, o_ = heapq.heappop(pe_)
                    heapq.heappush(av, (i_, o_))
                if av:
                    cand = (free[e], av[0][0], e, 0)
                elif pe_:
                    cand = (pe_[0][0], pe_[0][1], e, 1)
                else:
                    continue
                if best is None or cand < best:
                    best = cand
            start, _, e, src = best
            if src == 0:
                _, op = heapq.heappop(avail[e])
            else:
                _, _, op = heapq.heappop(pend[e])
            if op.dma_key is not None:
                free[e] = start + 0.06
                f_ = start + op.cost
            else:
                free[e] = start + op.cost
                f_ = start + op.cost
            sched[e].append(op)
            n += 1
            for s_, oo in op.succ:
                if oo:
                    t_ = free[e] if op.dma_key is not None else f_
                else:
                    t_ = f_ + (0.05 if (s_.eng == op.eng and op.dma_key is None) else SEM_LAT)
                if t_ > s_.rt:
                    s_.rt = t_
                s_.nd -= 1
                if s_.nd == 0:
                    heapq.heappush(pend[s_.eng], (s_.rt, s_.idx, s_))
        self.ops = sched
        self.est = max(free.values())
        if CRITPATH:
            for op in ops:
                best, bp = 0.0, None
                ds = [d for d, _ in op.deps] + list(op.odeps)
                for d in ds:
                    lat = 0.05 if (d.eng == op.eng and d.dma_key is None) else SEM_LAT
                    if d.ef + lat > best:
                        best, bp = d.ef + lat, d
                op.ef = best + op.cost
                op.cp = bp
            last = max(ops, key=lambda o: o.ef)
            print(f"[critpath {self.name}] dependency-only critical path = {last.ef:.0f} us")
            path = []
            o = last
            while o is not None:
                path.append(o)
                o = o.cp
            path.reverse()
            import collections
            cnt = collections.Counter((o.eng, o.tag) for o in path)
            print("   top path contributors (eng, line, count):", cnt.most_common(25))

    def finish(self):
        prog = self.prog
        if SCHEDULE:
            self.schedule()
        for e in ENGS:
            c = prog.engcnt[e]
            for op in self.ops[e]:
                if op.dma_key is None and op.sig:
                    c += 1
                    op.cnt = c
            prog.engcnt[e] = c
        stage = self

        def emit(eng_name, e):
            waited = {}
            for op in stage.ops[eng_name]:
                for d, dv in op.deps:
                    if d.dma_key is not None:
                        sem = stage.dma_ents[d.dma_key][0]
                        val = 16 * dv
                    else:
                        sem = prog.engsem[d.eng]
                        val = d.cnt
                    k = id(sem)
                    if waited.get(k, 0) < val:
                        e.wait_ge(sem, val)
                        waited[k] = val
                inst = op.fn(e)
                if op.dma_key is not None:
                    inst.then_inc(stage.dma_ents[op.dma_key][0], 16)
                elif op.sig:
                    inst.then_inc(prog.engsem[eng_name], 1)
            finals = {}
            for op in stage.ops[eng_name]:
                if op.dma_key is not None:
                    ent = stage.dma_ents[op.dma_key]
                    finals[id(ent[0])] = (ent[0], 16 * ent[1])
            for sem, val in finals.values():
                if waited.get(id(sem), 0) < val:
                    e.wait_ge(sem, val)

        with self.nc.Block() as block:
            @block.sync
            def _(e):
                emit("sync", e)

            @block.scalar
            def _(e):
                emit("act", e)

            @block.vector
            def _(e):
                emit("dve", e)

            @block.gpsimd
            def _(e):
                emit("pool", e)

            @block.tensor
            def _(e):
                emit("pe", e)
        for ent in self.dma_ents.values():
            prog.dma_free.setdefault(ent[2], []).append(ent)
        self.es.close()
        n = {e: len(self.ops[e]) for e in ENGS}
        print(f"[stage {self.name}] ops: {n} est_us={getattr(self, 'est', 0):.0f}")


T = 4096
D = 1024
NT = T // 128
ALPHA = 4 ** 0.25
LN_EPS = 1e-5
NEG = -30000.0


def load_bcast(st, eng, dst, src_row_ap, n, key, res):
    st.dma(eng, dst, src_row_ap.broadcast_to([128, n]), writes=[res], key=key)


def stage_proj(prog, name, x_d, w_d, col_lo, ncols, groups, consts):
    st = Stage(prog, name)
    nc = st.nc
    xT = st.sb("xT", [128, 8, T], BF16)
    wb = st.sb("wb", [128, 8, ncols], BF16)
    idf = st.sb("idf", [128, 128], F32)
    cos = st.sb("cos", [32, T], F32)
    sin = st.sb("sin", [32, T], F32)
    psw = st.sb("psw", [128, 128], BF16)
    xin = [st.sb(f"xin{i}", [128, D], F32) for i in range(2)]
    pA = [st.ps(f"pA{i}", [128, 512], F32) for i in range(2)]
    pB = [st.ps(f"pB{i}", [128, 512], F32) for i in range(2)]
    pC = [st.ps(f"pC{i}", [128, 512], F32) for i in range(2)]
    NQ = 3
    qt = [st.sb(f"qt{i}", [128, 512], BF16) for i in range(NQ)]
    t1 = [st.sb(f"t1{i}", [32, 512], F32) for i in range(2)]
    t2 = [st.sb(f"t2{i}", [32, 512], F32) for i in range(2)]
    NS = 3
    tms_f = [st.sb(f"tmsf{i}", [128, 512], F32) for i in range(NS)]
    tms_b = [st.sb(f"tmsb{i}", [128, 512], BF16) for i in range(NS)]

    st.dma("sync", idf[:], consts["ident"][:, :], writes=["idf"], key="c0")
    st.dma("sync", cos[:], consts["cos"][:, :], writes=["cos"], key="c0")
    st.dma("sync", sin[:], consts["sin"][:, :], writes=["sin"], key="c0")
    st.dma("pool", psw[:], consts["pswap"][:, :], writes=["psw"], key="c1")
    wv = w_d.rearrange("(kc p) n -> p kc n", p=128)
    for kc in range(8):
        st.dma("pool", wb[:, kc, :], wv[:, kc, col_lo:col_lo + ncols], writes=[("wb", kc)], key="wb")

    for ti in range(NT):
        s = ti % 2
        st.dma("sync", xin[s][:], x_d[ti * 128:(ti + 1) * 128, :], writes=[("xin", s)], key=("xin", s))
        for half in range(2):
            p = pA[(2 * ti + half) % 2]
            pk = ("pA", (2 * ti + half) % 2)
            for q in range(4):
                kc = half * 4 + q
                st.op("pe", lambda e, p=p, q=q, s=s, kc=kc: e.transpose(p[:, q * 128:(q + 1) * 128], xin[s][:, kc * 128:(kc + 1) * 128], idf[:]),
                      reads=[("xin", s), "idf"], writes=[pk])
            eng = "act" if half == 0 else "dve"
            if eng == "act":
                st.op("act", lambda e, p=p, half=half, ti=ti: e.copy(out=xT[:, half * 4:half * 4 + 4, ti * 128:(ti + 1) * 128], in_=p[:].rearrange("p (a b) -> p a b", a=4)),
                      reads=[pk], writes=[("xT", ti)])
            else:
                st.op("dve", lambda e, p=p, half=half, ti=ti: e.tensor_copy(out=xT[:, half * 4:half * 4 + 4, ti * 128:(ti + 1) * 128], in_=p[:].rearrange("p (a b) -> p a b", a=4)),
                      reads=[pk], writes=[("xT", ti)])

    fmi = 0
    tmi = 0
    rci = 0
    for tt in range(T // 512):
        t0 = tt * 512
        xres = [("xT", 4 * tt + i) for i in range(4)]
        for g in groups:
            if g["kind"] in ("fm", "fm_rot"):
                c0 = g["col"]
                p = pA[fmi % 2]
                pk = ("pA", fmi % 2)
                for kc in range(8):
                    st.op("pe", lambda e, p=p, kc=kc, c0=c0, t0=t0: e.matmul(p[:], lhsT=wb[:, kc, c0:c0 + 128], rhs=xT[:, kc, t0:t0 + 512], start=(kc == 0), stop=(kc == 7)),
                          reads=xres + [("wb", kc)], writes=[pk])
                qs = fmi % NQ
                qk = ("qt", qs)
                q_ = qt[qs]
                st.op("act", lambda e, p=p, q_=q_: e.copy(out=q_[:], in_=p[:]), reads=[pk], writes=[qk, pk])
                if g["kind"] == "fm_rot":
                    pb = pB[rci % 2]
                    pbk = ("pB", rci % 2)
                    a1 = t1[rci % 2]
                    a2 = t2[rci % 2]
                    a1k = ("t1", rci % 2)
                    a2k = ("t2", rci % 2)
                    st.op("pe", lambda e, pb=pb, q_=q_: e.matmul(pb[:, :], lhsT=psw[:, :], rhs=q_[:, :], start=True, stop=True),
                          reads=[qk, "psw"], writes=[pbk])
                    st.op("dve", lambda e, p=p, a1=a1, t0=t0: e.tensor_tensor(out=a1[:], in0=p[0:32, :], in1=cos[:, t0:t0 + 512], op=ALU.mult),
                          reads=[pk, "cos"], writes=[a1k, pk])
                    st.op("dve", lambda e, pb=pb, a2=a2, t0=t0: e.tensor_tensor(out=a2[:], in0=pb[0:32, :], in1=sin[:, t0:t0 + 512], op=ALU.mult),
                          reads=[pbk, "sin"], writes=[a2k])
                    st.op("pool", lambda e, q_=q_, a1=a1, a2=a2: e.tensor_tensor(out=q_[0:32, :], in0=a1[:], in1=a2[:], op=ALU.add),
                          reads=[a1k, a2k, qk], writes=[qk])
                    rci += 1
                st.dma("sync", g["dst"][:, t0:t0 + 512], q_[:], reads=[qk], key=("qo", qs))
                fmi += 1
            else:
                c0 = g["col"]
                n = g["n"]
                for sub in range(4):
                    ta = t0 + sub * 128
                    p = pC[tmi % 2]
                    pk = ("pC", tmi % 2)
                    for kc in range(8):
                        st.op("pe", lambda e, p=p, kc=kc, c0=c0, n=n, ta=ta: e.matmul(p[:, 0:n], lhsT=xT[:, kc, ta:ta + 128], rhs=wb[:, kc, c0:c0 + n], start=(kc == 0), stop=(kc == 7)),
                              reads=[("xT", ta // 128), ("wb", kc)], writes=[pk])
                    ss = tmi % NS
                    stg = tms_f[ss] if g["dt"] == F32 else tms_b[ss]
                    sk = ("tms", ss)
                    eng = "act" if tmi % 2 == 0 else "dve"
                    if eng == "act":
                        st.op("act", lambda e, p=p, stg=stg, n=n: e.copy(out=stg[:, 0:n], in_=p[:, 0:n]), reads=[pk], writes=[sk])
                    else:
                        st.op("dve", lambda e, p=p, stg=stg, n=n: e.tensor_copy(out=stg[:, 0:n], in_=p[:, 0:n]), reads=[pk], writes=[sk])
                    if "dst_fn" in g:
                        o_ap, i_ap = g["dst_fn"](ta, stg)
                        st.dma("sync", o_ap, i_ap, reads=[sk], key=("to", ss))
                    else:
                        st.dma("sync", g["dst"][ta:ta + 128, g.get("dcol", 0):g.get("dcol", 0) + n], stg[:, 0:n], reads=[sk], key=("to", ss))
                    tmi += 1
    st.finish()


def layer_norm_tile(st, u, uk, outt, outk, gam, bet, tag, slot):
    stats, mv, rstd = st.lnbuf[slot]
    sk = ("lnstat", slot)
    for c in range(2):
        st.op("dve", lambda e, c=c: e.bn_stats(out=stats[:, c * 6:(c + 1) * 6], in_=u[:, c * 512:(c + 1) * 512]), reads=[uk], writes=[sk])
    st.op("dve", lambda e: e.bn_aggr(out=mv[:], in_=stats[:]), reads=[sk], writes=[sk])
    st.op("dve", lambda e: e.tensor_scalar_add(out=rstd[:], in0=mv[:, 1:2], scalar1=LN_EPS), reads=[sk], writes=[sk])
    st.op("act", lambda e: e.sqrt(out=rstd[:], in_=rstd[:]), reads=[sk], writes=[sk])
    st.op("dve", lambda e: e.reciprocal(out=rstd[:], in_=rstd[:]), reads=[sk], writes=[sk])
    st.op("dve", lambda e: e.tensor_scalar(out=outt[:], in0=u[:], scalar1=mv[:, 0:1], scalar2=rstd[:, 0:1], op0=ALU.subtract, op1=ALU.mult),
          reads=[uk, sk], writes=[outk])
    st.op("pool", lambda e: e.tensor_tensor(out=outt[:], in0=outt[:], in1=gam[:], op=ALU.mult), reads=[outk, "lnc"], writes=[outk])
    st.op("pool", lambda e: e.tensor_tensor(out=outt[:], in0=outt[:], in1=bet[:], op=ALU.add), reads=[outk, "lnc"], writes=[outk])


def stage_postA(prog, name, yT_d, xres_d, wout_d, g1_d, b1_d, x1_d, x1T_d, consts):
    st = Stage(prog, name)
    wo = st.sb("wo", [128, 8, D], BF16)
    idf = st.sb("idf", [128, 128], F32)
    g1 = st.sb("g1", [128, D], F32)
    b1 = st.sb("b1", [128, D], F32)
    yt = [st.sb(f"yt{i}", [128, 8, 512], BF16) for i in range(2)]
    xr = [st.sb(f"xr{i}", [128, D], F32) for i in range(2)]
    u = [st.sb(f"u{i}", [128, D], F32) for i in range(3)]
    xT = [st.sb(f"xT{i}", [128, 8, 512], BF16) for i in range(2)]
    st.lnbuf = [(st.sb(f"lns{i}", [128, 12], F32), st.sb(f"lnm{i}", [128, 2], F32), st.sb(f"lnr{i}", [128, 1], F32)) for i in range(3)]
    ph = [st.ps(f"ph{i}", [128, 512], F32) for i in range(4)]
    pt = [st.ps(f"pt{i}", [128, 512], F32) for i in range(2)]
    st.dma("sync", idf[:], consts["ident"][:, :], writes=["idf"], key="c0")
    for (dst, src) in ((g1, g1_d), (b1, b1_d)):
        st.dma("sync", dst[:], src.broadcast_to([128, D]), writes=["lnc"], key="c0")
    wov = wout_d.rearrange("(kc p) n -> p kc n", p=128)
    for kc in range(8):
        st.dma("pool", wo[:, kc, :], wov[:, kc, :], writes=[("wo", kc)], key="wo")
    yTv = yT_d.rearrange("(kc p) t -> p kc t", p=128)
    x1Tv = x1T_d.rearrange("(kc p) t -> p kc t", p=128)
    for ti in range(NT):
        ta = ti * 128
        s2 = ti % 2
        s3 = ti % 3
        g4 = (ti // 4) % 2
        sub = ti % 4
        if sub == 0:
            st.dma("sync", yt[g4][:], yTv[:, :, ta:ta + 512], writes=[("yt", g4)], key=("yt", g4))
        st.dma("sync", xr[s2][:], xres_d[ta:ta + 128, :], writes=[("xr", s2)], key=("xr", s2))
        for c in range(2):
            pi = 2 * s2 + c
            for kc in range(8):
                st.op("pe", lambda e, c=c, kc=kc, g4=g4, sub=sub, pi=pi: e.matmul(ph[pi][:], lhsT=yt[g4][:, kc, sub * 128:(sub + 1) * 128], rhs=wo[:, kc, c * 512:(c + 1) * 512], start=(kc == 0), stop=(kc == 7)),
                      reads=[("yt", g4), ("wo", kc)], writes=[("ph", pi)])
            st.op("dve", lambda e, c=c, s2=s2, s3=s3, pi=pi: e.scalar_tensor_tensor(out=u[s3][:, c * 512:(c + 1) * 512], in0=xr[s2][:, c * 512:(c + 1) * 512], scalar=ALPHA, in1=ph[pi][:], op0=ALU.mult, op1=ALU.add),
                  reads=[("xr", s2), ("ph", pi)], writes=[("u", s3)])
        layer_norm_tile(st, u[s3], ("u", s3), u[s3], ("u", s3), g1, b1, "ln1", s3)
        st.dma("sync", x1_d[ta:ta + 128, :], u[s3][:], reads=[("u", s3)], key=("x1o", s3))
        for half in range(2):
            p = pt[half]
            for q in range(4):
                kc = half * 4 + q
                st.op("pe", lambda e, p=p, q=q, kc=kc, s3=s3: e.transpose(p[:, q * 128:(q + 1) * 128], u[s3][:, kc * 128:(kc + 1) * 128], idf[:]),
                      reads=[("u", s3), "idf"], writes=[("pt", half)])
            st.op("act", lambda e, p=p, half=half, g4=g4, sub=sub: e.copy(out=xT[g4][:, half * 4:half * 4 + 4, sub * 128:(sub + 1) * 128], in_=p[:].rearrange("p (a b) -> p a b", a=4)),
                  reads=[("pt", half)], writes=[("xT", g4)])
        if sub == 3:
            st.dma("sync", x1Tv[:, :, ta - 384:ta + 128], xT[g4][:], reads=[("xT", g4)], key=("xTo", g4))
    st.finish()


def stage_mlp(prog, name, x1_d, x1T_d, w1_d, w2_d, g2_d, b2_d, out_d):
    st = Stage(prog, name)
    TB = 256
    w1 = st.sb("w1", [128, 8, 4 * D], BF16)
    w2 = st.sb("w2", [128, 32, D], BF16)
    g2 = st.sb("g2", [128, D], F32)
    b2 = st.sb("b2", [128, D], F32)
    x1T = [st.sb(f"x1T{i}", [128, 8, 512], BF16) for i in range(2)]
    x1 = [st.sb(f"x1_{i}", [128, D], F32) for i in range(2)]
    aT = st.sb("aT", [128, 32, TB], BF16)
    rl = [st.sb(f"rl{i}", [128, TB], F32) for i in range(2)]
    v_ = [st.sb(f"v{i}", [128, D], F32) for i in range(2)]
    st.lnbuf = [(st.sb(f"lns{i}", [128, 12], F32), st.sb(f"lnm{i}", [128, 2], F32), st.sb(f"lnr{i}", [128, 1], F32)) for i in range(2)]
    pm = [st.ps(f"pm{i}", [128, 512], F32) for i in range(2)]
    po = [st.ps(f"po{i}", [128, 512], F32) for i in range(4)]
    for (dst, src) in ((g2, g2_d), (b2, b2_d)):
        st.dma("sync", dst[:], src.broadcast_to([128, D]), writes=["lnc"], key="c0")
    w1v = w1_d.rearrange("(kc p) n -> p kc n", p=128)
    for cb_ in range(4):
        for kc in range(8):
            st.dma("pool", w1[:, kc, cb_ * 1024:(cb_ + 1) * 1024], w1v[:, kc, cb_ * 1024:(cb_ + 1) * 1024], writes=[("w1", cb_)], key=("w1", cb_))
    w2v = w2_d.rearrange("(hc p) n -> p hc n", p=128)
    for hc in range(32):
        st.dma("pool", w2[:, hc, :], w2v[:, hc, :], writes=[("w2", hc // 8)], key=("w2", hc // 8))
    x1Tv = x1T_d.rearrange("(kc p) t -> p kc t", p=128)
    oi = 0
    for tt in range(T // TB):
        t0 = tt * TB
        sx = (tt // 2) % 2
        hx = (tt % 2) * TB
        if tt % 2 == 0:
            st.dma("sync", x1T[sx][:], x1Tv[:, :, t0:t0 + 512], writes=[("x1T", sx)], key=("x1T", sx))
        for hc in range(32):
            p = pm[hc % 2]
            pk = ("pm", hc % 2)
            for kc in range(8):
                st.op("pe", lambda e, p=p, kc=kc, hc=hc, sx=sx, hx=hx: e.matmul(p[:, 0:TB], lhsT=w1[:, kc, hc * 128:(hc + 1) * 128], rhs=x1T[sx][:, kc, hx:hx + TB], start=(kc == 0), stop=(kc == 7)),
                      reads=[("x1T", sx), ("w1", hc // 8)], writes=[pk])
            r_ = rl[hc % 2]
            rk = ("rl", hc % 2)
            st.op("act", lambda e, p=p, r_=r_: e.activation(out=r_[:], in_=p[:, 0:TB], func=AF.Relu), reads=[pk], writes=[rk])
            st.op("pool", lambda e, r_=r_, hc=hc: e.tensor_tensor(out=aT[:, hc, :], in0=r_[:], in1=r_[:], op=ALU.mult), reads=[rk], writes=[("aT", hc)])
        for sub in range(TB // 128):
            ta = t0 + sub * 128
            s2 = oi % 2
            st.dma("sync", x1[s2][:], x1_d[ta:ta + 128, :], writes=[("x1", s2)], key=("x1", s2))
            for c in range(2):
                pi = 2 * s2 + c
                for hc in range(32):
                    st.op("pe", lambda e, c=c, hc=hc, sub=sub, pi=pi: e.matmul(po[pi][:], lhsT=aT[:, hc, sub * 128:(sub + 1) * 128], rhs=w2[:, hc, c * 512:(c + 1) * 512], start=(hc == 0), stop=(hc == 31)),
                          reads=[("aT", hc), ("w2", hc // 8)], writes=[("po", pi)])
                st.op("dve", lambda e, c=c, s2=s2, pi=pi: e.scalar_tensor_tensor(out=v_[s2][:, c * 512:(c + 1) * 512], in0=x1[s2][:, c * 512:(c + 1) * 512], scalar=ALPHA, in1=po[pi][:], op0=ALU.mult, op1=ALU.add),
                      reads=[("x1", s2), ("po", pi)], writes=[("v", s2)])
            layer_norm_tile(st, v_[s2], ("v", s2), v_[s2], ("v", s2), g2, b2, "ln2", s2)
            st.dma("sync", out_d[ta:ta + 128, :], v_[s2][:], reads=[("v", s2)], key=("vo", s2))
            oi += 1
    st.finish()


SCALE = 128 ** -0.5
DIL = (1, 4, 16)


def stage_dil(prog, name, dq_d, dk_d, dv_d, oT_d, consts, nS=4, nN=4, NP=4):
    st = Stage(prog, name)
    st.costs = {"pe": 0.09, "act": 0.36, "dve": 0.66, "pool": 2.8}
    identb = st.sb("identb", [128, 128], BF16)
    onesb = st.sb("onesb", [128, 128], BF16)
    mask = st.sb("mask", [128, 256], BF16)
    kT = [st.sb(f"kT{i}", [128, T], BF16) for i in range(2)]
    qT = [[st.sb(f"qT{i}_{g}", [128, T], BF16) for g in range(3)] for i in range(2)]
    Vd = [[st.sb(f"V{i}_{g}", [128, T // 128 // DIL[g], DIL[g], 128], BF16) for g in range(3)] for i in range(2)]
    accs = [st.sb(f"acc{i}", [128, 2, T], F32) for i in range(2)]
    rd = st.sb("rd", [128, 1024], F32)
    ot = [st.sb(f"ot{i}", [128, 1024], BF16) for i in range(2)]
    pT = [st.sb(f"pT{i}", [128, 512], BF16) for i in range(NP)]
    pS = [st.ps(f"pS{i}", [128, 512], F32) for i in range(nS)]
    pN = [st.ps(f"pN{i}", [128, 512], F32) for i in range(nN)]
    st.dma("pool", identb[:], consts["ident"][:, :], writes=["c"], key="c0")
    st.dma("pool", onesb[:], consts["ones"][:, :], writes=["c"], key="c0")
    st.dma("pool", mask[:], consts["dmask"][:, :], writes=["c"], key="c0")
    bi = 0
    oi = 0
    for h in range(8):
        sl = h % 2
        lk = ("ld", sl)
        acc = accs[h % 2]
        st.dma("sync", kT[sl][:], dk_d[h], writes=[lk], key=lk)
        for g in range(3):
            st.dma("sync", qT[sl][g][:], dq_d[g * 8 + h], writes=[lk], key=lk)
        for g, d in enumerate(DIL):
            src = dv_d[h].rearrange("(n p r) c -> p n (r c)", p=128, r=d)
            st.dma("sync", Vd[sl][g][:].rearrange("p n r c -> p n (r c)"), src, writes=[lk], key=lk)
        for g, d in enumerate(DIL):
            nblk = T // d // 128
            for r in range(d):
                for n0 in range(0, nblk, 2):
                    def cols(nn, r=r, d=d, cnt=128):
                        s0 = r + d * 128 * nn
                        return slice(s0, s0 + d * (cnt - 1) + 1, d)
                    si = bi % nS
                    S = pS[si]
                    Sk = ("pS", si)
                    for u in range(2):
                        n = n0 + u
                        qv = qT[sl][g][:, cols(n)]
                        c0 = u * 256
                        if n > 0:
                            st.op("pe", lambda e, S=S, n=n, qv=qv, cols=cols, sl=sl, c0=c0: e.matmul(S[:, c0:c0 + 128], lhsT=kT[sl][:, cols(n - 1)], rhs=qv, start=True, stop=False),
                                  reads=[lk], writes=[Sk])
                            st.op("pe", lambda e, S=S, c0=c0: e.matmul(S[:, c0:c0 + 128], lhsT=identb[:], rhs=mask[:, 0:128], start=False, stop=True),
                                  reads=["c"], writes=[Sk])
                        st.op("pe", lambda e, S=S, n=n, qv=qv, cols=cols, sl=sl, c0=c0: e.matmul(S[:, c0 + 128:c0 + 256], lhsT=kT[sl][:, cols(n)], rhs=qv, start=True, stop=False),
                              reads=[lk], writes=[Sk])
                        st.op("pe", lambda e, S=S, c0=c0: e.matmul(S[:, c0 + 128:c0 + 256], lhsT=identb[:], rhs=mask[:, 128:256], start=False, stop=True),
                              reads=["c"], writes=[Sk])
                    lo = 0 if n0 > 0 else 128
                    pi = bi % NP
                    P = pT[pi]
                    Pk = ("pT", pi)
                    st.op("act", lambda e, S=S, P=P, lo=lo: e.activation(out=P[:, lo:512], in_=S[:, lo:512], func=AF.Exp, scale=SCALE),
                          reads=[Sk], writes=[Pk, Sk], c=0.6)
                    ni = bi % nN
                    N_ = pN[ni]
                    Nk = ("pN", ni)
                    for half in range(2):
                        for u in range(2):
                            n = n0 + u
                            vb = r * nblk + n
                            o0 = half * 256 + u * 128
                            c0 = u * 256
                            if half == 0:
                                lp = Vd[sl][g][:, n - 1, r, :] if n > 0 else None
                                lc = Vd[sl][g][:, n, r, :]
                            else:
                                lp = onesb[:] if n > 0 else None
                                lc = onesb[:]
                            if n > 0:
                                st.op("pe", lambda e, N_=N_, o0=o0, lp=lp, P=P, c0=c0: e.matmul(N_[:, o0:o0 + 128], lhsT=lp, rhs=P[:, c0:c0 + 128], start=True, stop=False),
                                      reads=[Pk, lk, "c"], writes=[Nk])
                            st.op("pe", lambda e, N_=N_, o0=o0, lc=lc, P=P, n=n, c0=c0: e.matmul(N_[:, o0:o0 + 128], lhsT=lc, rhs=P[:, c0 + 128:c0 + 256], start=(n == 0), stop=True),
                                  reads=[Pk, lk, "c"], writes=[Nk])
                    av = acc[:, :, cols(n0, cnt=256)]
                    nv = N_[:, :].rearrange("p (a b) -> p a b", a=2)
                    n = n0
                    ablk = (n // 16) if g == 0 else ((n // 4) if g == 1 else 0)
                    acks = [("acc", h % 2, ablk)] if g < 2 else [("acc", h % 2, 0), ("acc", h % 2, 1)]
                    if g == 0:
                        st.op("act", lambda e, av=av, nv=nv: e.copy(out=av, in_=nv), reads=[Nk], writes=acks + [Nk], c=0.6)
                    else:
                        st.op("dve", lambda e, av=av, nv=nv: e.tensor_tensor(out=av, in0=nv, in1=av, op=ALU.add), reads=[Nk] + acks, writes=acks + [Nk], c=1.2)
                    bi += 1
        for c in range(4):
            cs = slice(c * 1024, (c + 1) * 1024)
            o_ = ot[oi % 2]
            ok = ("ot", oi % 2)
            ack = ("acc", h % 2, c // 2)
            st.op("act", lambda e, cs=cs, acc=acc: e.activation(out=rd[:], in_=acc[:, 1, cs], func=AF.Ln), reads=[ack], writes=["rd"], c=0.9)
            st.op("act", lambda e: e.activation(out=rd[:], in_=rd[:], func=AF.Exp, scale=-1.0), reads=["rd"], writes=["rd"], c=0.9)
            st.op("pool", lambda e, cs=cs, o_=o_, acc=acc: e.tensor_tensor(out=o_[:], in0=acc[:, 0, cs], in1=rd[:], op=ALU.mult), reads=[ack, "rd"], writes=[ok], c=3.0)
            st.dma("sync", oT_d[h * 128:(h + 1) * 128, cs], o_[:], reads=[ok], key=ok)
            oi += 1
    st.finish()


def stage_moba(prog, name, mq_d, mk_d, mv_d, yT_d, consts, st=None, nS=3, nN=2, nG=1, nld=2, finish=True, NP=4):
    if st is None:
        st = Stage(prog, name)
    identb = st.sb("identb", [128, 128], BF16)
    identf = st.sb("identf", [128, 128], F32)
    onesb = st.sb("onesb", [128, 128], BF16)
    cmask = st.sb("cmask", [128, 2, 256], BF16)
    selall = st.sb("selall", [16, 16 * 128], BF16)
    kT = [st.sb(f"kT{i}", [128, T], BF16) for i in range(nld)]
    qT = [st.sb(f"qT{i}", [128, T], BF16) for i in range(nld)]
    V = [st.sb(f"V{i}", [128, 32, 128], BF16) for i in range(nld)]
    nmT = st.sb("nmT", [16, T], BF16)
    ks = st.sb("ks", [128, 16], F32)
    khi = st.sb("khi", [128, 16], BF16)
    klo = st.sb("klo", [128, 16], BF16)
    gm = [st.sb(f"gm{i}", [128, 16], F32) for i in range(2)]
    m8 = [st.sb(f"m8{i}", [128, 8], F32) for i in range(2)]
    nm = [st.sb(f"nm{i}", [128, 16], F32) for i in range(2)]
    rd = st.sb("rd", [128, 256], F32)
    dacc = [st.sb(f"dacc{i}", [128, 256], F32) for i in range(2)]
    daccb = [st.sb(f"daccb{i}", [128, 256], BF16) for i in range(2)]
    daccl = [st.sb(f"daccl{i}", [128, 256], BF16) for i in range(2)]
    ot = [st.sb(f"ot{i}", [128, 256], BF16) for i in range(2)]
    pT = [st.sb(f"pT{i}", [128, 256], BF16) for i in range(NP)]
    pS = [st.ps(f"pS{i}", [128, 512], F32) for i in range(nS)]
    pNum = [st.ps(f"pNu{i}", [128, 512], F32) for i in range(nN)]
    pDen = [st.ps(f"pDe{i}", [128, 512], F32) for i in range(nN)]
    pG = [st.ps(f"pG{i}", [128, 512], F32) for i in range(nG)]
    st.dma("pool", identb[:], consts["ident"][:, :], writes=["c"], key="c0")
    st.dma("sync", identf[:], consts["ident"][:, :], writes=["c"], key="c1")
    st.dma("pool", onesb[:], consts["ones"][:, :], writes=["c"], key="c0")
    st.dma("pool", cmask[:], consts["cmask"][:, :, :], writes=["c"], key="c0")
    st.dma("pool", selall[:], consts["selall"][:, :], writes=["c"], key="c0")
    bi = 0
    gi = 0
    for h in range(4):
        sl = h % nld
        lk = ("ld", sl)
        st.dma("sync", kT[sl][:], mk_d[h], writes=[lk], key=lk)
        st.dma("sync", qT[sl][:], mq_d[h], writes=[lk], key=lk)
        st.dma("sync", V[sl][:], mv_d[:, h * 128:(h + 1) * 128].rearrange("(n p) c -> p n c", p=128), writes=[lk], key=lk)
        st.op("dve", lambda e, sl=sl: e.tensor_reduce(out=ks[:], in_=kT[sl][:].rearrange("p (n k) -> p n k", k=256), axis=AX.X, op=ALU.add), reads=[lk], writes=["ks"])
        st.op("dve", lambda e: e.tensor_copy(out=khi[:], in_=ks[:]), reads=["ks"], writes=["khi"])
        st.op("dve", lambda e: e.tensor_tensor(out=klo[:], in0=ks[:], in1=khi[:], op=ALU.subtract), reads=["ks", "khi"], writes=["klo"])
        for qt in range(2, 32):
            i = qt // 2
            G = pG[gi % nG]
            Gk = ("pG", gi % nG)
            g_ = gm[gi % 2]
            m_ = m8[gi % 2]
            n_ = nm[gi % 2]
            sk = ("gs", gi % 2)
            st.op("pe", lambda e, G=G, qt=qt, sl=sl: e.matmul(G[:, 0:16], lhsT=qT[sl][:, qt * 128:(qt + 1) * 128], rhs=khi[:], start=True, stop=False), reads=[lk, "khi"], writes=[Gk])
            st.op("pe", lambda e, G=G, qt=qt, sl=sl: e.matmul(G[:, 0:16], lhsT=qT[sl][:, qt * 128:(qt + 1) * 128], rhs=klo[:], start=False, stop=True), reads=[lk, "klo"], writes=[Gk])
            st.op("dve", lambda e, G=G, g_=g_, i=i: e.tensor_copy(out=g_[:, 0:i], in_=G[:, 0:i]), reads=[Gk], writes=[sk, Gk])
            if i < 16:
                st.op("dve", lambda e, g_=g_, i=i: e.memset(g_[:, i:16], -1e30), reads=[], writes=[sk])
            st.op("dve", lambda e, g_=g_, m_=m_: e.max(out=m_[:], in_=g_[:]), reads=[sk], writes=[sk])
            st.op("dve", lambda e, g_=g_, m_=m_, n_=n_: e.tensor_scalar(out=n_[:], in0=g_[:], scalar1=m_[:, 2:3], scalar2=NEG, op0=ALU.is_lt, op1=ALU.mult), reads=[sk], writes=[sk])
            st.op("pe", lambda e, G=G, n_=n_: e.transpose(G[0:16, 128:256], n_[:, :], identf[:]), reads=[sk, "c"], writes=[Gk])
            st.op("act", lambda e, G=G, qt=qt: e.copy(out=nmT[:, qt * 128:(qt + 1) * 128], in_=G[0:16, 128:256]), reads=[Gk], writes=[("nmT", qt // 2), Gk])
            gi += 1
        for i in range(16):
            qs = slice(i * 256, (i + 1) * 256)
            Nu = pNum[i % nN]
            De = pDen[i % nN]
            Nuk = ("pNu", i % nN)
            Dek = ("pDe", i % nN)
            nkc = 2 * (i + 1)
            for j in range(i + 1):
                for c in range(2):
                    kc = 2 * j + c
                    si = bi % nS
                    S = pS[si]
                    Sk = ("pS", si)
                    st.op("pe", lambda e, S=S, kc=kc, qs=qs, sl=sl: e.matmul(S[:, 0:256], lhsT=kT[sl][:, kc * 128:(kc + 1) * 128], rhs=qT[sl][:, qs], start=True, stop=False),
                          reads=[lk], writes=[Sk])
                    if j < i:
                        st.op("pe", lambda e, S=S, j=j, qs=qs: e.matmul(S[:, 0:256], lhsT=selall[:, j * 128:(j + 1) * 128], rhs=nmT[:, qs], start=False, stop=True),
                              reads=["c", ("nmT", i)], writes=[Sk])
                    else:
                        st.op("pe", lambda e, S=S, c=c: e.matmul(S[:, 0:256], lhsT=identb[:], rhs=cmask[:, c, :], start=False, stop=True),
                              reads=["c"], writes=[Sk])
                    pi = bi % NP
                    P = pT[pi]
                    Pk = ("pT", pi)
                    st.op("act", lambda e, S=S, P=P: e.activation(out=P[:], in_=S[:, 0:256], func=AF.Exp, scale=SCALE), reads=[Sk], writes=[Pk, Sk])
                    st.op("pe", lambda e, Nu=Nu, kc=kc, P=P, sl=sl, nkc=nkc: e.matmul(Nu[:, 0:256], lhsT=V[sl][:, kc, :], rhs=P[:], start=(kc == 0), stop=(kc == nkc - 1)),
                          reads=[Pk, lk], writes=[Nuk])
                    st.op("pe", lambda e, De=De, kc=kc, P=P, nkc=nkc: e.matmul(De[:, 0:256], lhsT=onesb[:], rhs=P[:], start=(kc == 0), stop=(kc == nkc - 1)),
                          reads=[Pk, "c"], writes=[Dek])
                    bi += 1
            o_ = ot[i % 2]
            ok = ("ot", i % 2)
            st.op("dve", lambda e, De=De: e.reciprocal(out=rd[:], in_=De[:, 0:256]), reads=[Dek], writes=["rd", Dek])
            st.op("dve", lambda e, Nu=Nu, o_=o_: e.tensor_tensor(out=o_[:], in0=Nu[:, 0:256], in1=rd[:], op=ALU.mult), reads=[Nuk, "rd"], writes=[ok, Nuk])
            st.dma("sync", yT_d[512 + h * 128:512 + (h + 1) * 128, qs], o_[:], reads=[ok], key=ok)
    if finish:
        st.finish()


NCH = T // 128
DEC = -0.6065306597126334
GN_EPS = 64e-5
STOP = 0


def stage_rwkv(prog, name, zr_d, prm, yT_d, consts, nch=NCH, st=None, nbank=8, finish=True):
    if st is None:
        st = Stage(prog, name)
    nc = st.nc

    st.costs = {"act": 0.55, "dve": 0.8, "pool": 1.5, "pe": 0.15}

    def cb(nm, src, n):
        t = st.sb(nm, [128, n], F32)
        st.dma("sync", t[:], src.broadcast_to([128, n]), writes=["c"], key="c0")
        return t
    mixb = cb("mixb", prm["shift_mix"], 1824)
    w0b = cb("w0b", prm["w0"], 512)
    a0b = cb("a0b", prm["a0"], 512)
    kkb = cb("kkb", prm["k_k"], 512)
    kab = cb("kab", prm["k_a"], 512)
    rkb = cb("rkb", prm["r_k"], 512)
    lgb = cb("lgb", prm["lnx_g"], 512)
    lbb = cb("lbb", prm["lnx_b"], 512)
    Wwa = st.sb("Wwa", [128, 512], F32)
    Wg = st.sb("Wg", [128, 2, 512], F32)
    st.dma("sync", Wwa[0:64, :], prm["w_up"][:, :], writes=["c"], key="c0")
    st.dma("sync", Wwa[64:128, :], prm["a_up"][:, :], writes=["c"], key="c0")
    st.dma("sync", Wg[:, 0, :], prm["g_up"][0:128, :], writes=["c"], key="c0")
    st.dma("sync", Wg[0:32, 1, :], prm["g_up"][128:160, :], writes=["c"], key="c0")
    identf = st.sb("identf", [128, 128], F32)
    identb = st.sb("identb", [128, 128], BF16)
    tri = st.sb("tri", [128, 128], F32)
    onesf = st.sb("onesf", [128, 128], F32)
    mk12 = st.sb("mk12", [128, 512], BF16)
    mk3 = st.sb("mk3", [128, 512], BF16)
    st.dma("sync", identf[:], consts["ident"][:, :], writes=["c"], key="c0")
    st.dma("pool", identb[:], consts["ident"][:, :], writes=["c"], key="c1")
    st.dma("sync", tri[:], consts["tri"][:, :], writes=["c"], key="c0")
    st.dma("sync", onesf[:], consts["ones"][:, :], writes=["c"], key="c0")
    st.dma("pool", mk12[:], consts["mk12"][:, :], writes=["c"], key="c1")
    st.dma("pool", mk3[:], consts["mk3"][:, :], writes=["c"], key="c1")

    H = [st.sb(f"H{i}", [128, 64], F32) for i in range(4)]
    Hb = [st.sb(f"Hb{i}", [128, 64], BF16) for i in range(4)]
    for i in range(4):
        st.op("pool", lambda e, i=i: e.memset(H[i][:], 0.0), writes=[("H", i)])
        st.op("pool", lambda e, i=i: e.memset(Hb[i][:], 0.0), writes=[("Hb", i)])

    PB = [st.ps(f"pb{i}", [128, 512], F32) for i in range(nbank)]
    POOLS = {"p": [0, 1, 2], "i0": [3, 4, 5], "i1": [3, 4, 5], "s": [6, 7]} if nbank >= 8 else {"p": list(range(nbank)), "i": list(range(nbank)), "s": list(range(nbank))}
    bank_ctr = {"p": 0, "i0": 0, "i1": 0, "s": 0}
    cur_pool = ["p"]

    def bank():
        pl = cur_pool[0]
        if nbank < 8:
            k_ = bank_ctr["p"]
            bank_ctr["p"] += 1
            i = k_ % nbank
        else:
            lst = POOLS[pl]
            i = lst[bank_ctr[pl] % len(lst)]
            bank_ctr[pl] += 1
        return PB[i], ("pb", i)

    tiles = {}

    NBUF = {}

    def tl(nm, shape, dt, nb=2):
        t_ = [st.sb(f"{nm}{p}", shape, dt) for p in range(nb)]
        tiles[nm] = [t_[p % nb] for p in range(2)]
        NBUF[nm] = nb
    single = ("t0", "t1", "t2", "Ysq", "kka", "LT", "sm", "gam", "bon", "Ysb")
    for nm in ("z", "zp", "zs"):
        tl(nm, [128, 1824], F32, 2)
    tl("LT", [128, 384], F32, 1)
    for nm in ("sgw", "a", "g", "Ep", "En", "Ex", "Et", "t0", "t1", "t2", "kk", "k2", "kka", "Ysb", "Ysq", "ya", "bon"):
        tl(nm, [128, 512], F32, 1 if nm in single else 2)
    for nm in ("RT", "AT", "BT_", "KT_", "Bh", "Kh", "vb", "Zs", "Us"):
        tl(nm, [128, 512], BF16, 2)
    tl("ART", [128, 4, 2, 128], BF16)
    tl("BTT", [128, 4, 128], BF16)
    tl("KTT", [128, 4, 128], BF16)
    tl("ABT", [128, 8, 2, 128], BF16)
    tl("AKT", [128, 8, 2, 128], BF16)
    for nm in ("A0", "A1", "M0", "M1", "P0", "P1"):
        tl(nm, [128, 8, 128], BF16, 2)
    tl("sm", [128, 64], F32, 2)
    tl("gam", [128, 4], F32, 2)
    yaT4 = [st.sb(f"yaT4_{i}", [128, 4, 512], BF16) for i in range(2)]

    def do_chunk(c):
        p = c % 2
        t0 = c * 128
        X = {k: v[p] for k, v in tiles.items()}
        K = lambda nm: (nm, p if NBUF[nm] == 2 else 0)
        z, zp, zs = X["z"], X["zp"], X["zs"]
        st.dma("sync", z[:], zr_d[t0:t0 + 128, :], writes=[K("z")], key=("z", p))
        if c == 0:
            st.op("pool", lambda e, zp=zp: e.memset(zp[:], 0.0), writes=[K("zp")])
            st.dma("sync", zp[1:128, :], zr_d[0:127, :], reads=[K("zp")], writes=[K("zp")], key=("zp", p))
        else:
            st.dma("sync", zp[:], zr_d[t0 - 1:t0 + 127, :], writes=[K("zp")], key=("zp", p))
        st.op("dve", lambda e, z=z, zp=zp, zs=zs: e.tensor_tensor(out=zs[:], in0=zp[:], in1=z[:], op=ALU.subtract), reads=[K("z"), K("zp")], writes=[K("zs")])
        st.op("pool", lambda e, zs=zs: e.tensor_tensor(out=zs[:], in0=zs[:], in1=mixb[:], op=ALU.mult), reads=[K("zs"), "c"], writes=[K("zs")])
        st.op("dve", lambda e, z=z, zs=zs: e.tensor_tensor(out=zs[:], in0=zs[:], in1=z[:], op=ALU.add), reads=[K("zs"), K("z")], writes=[K("zs")])
        r = zs[:, 0:512]
        k = zs[:, 512:1024]
        v = zs[:, 1024:1536]
        st.op("act", lambda e, zs=zs: e.activation(out=zs[:, 1536:1600], in_=zs[:, 1536:1600], func=AF.Tanh), reads=[K("zs")], writes=[K("zs")])
        st.op("act", lambda e, zs=zs: e.activation(out=zs[:, 1664:1824], in_=zs[:, 1664:1824], func=AF.Sigmoid), reads=[K("zs")], writes=[K("zs")])
        cur_pool[0] = "p"
        bL, bLk = bank()
        st.op("pe", lambda e, bL=bL, zs=zs: e.transpose(bL[:, 0:128], zs[:, 1536:1664], identf[:]), reads=[K("zs"), "c"], writes=[bLk])
        st.op("pe", lambda e, bL=bL, zs=zs: e.transpose(bL[:, 128:256], zs[:, 1664:1792], identf[:]), reads=[K("zs"), "c"], writes=[bLk])
        st.op("pe", lambda e, bL=bL, zs=zs: e.transpose(bL[0:32, 256:384], zs[:, 1792:1824], identf[:]), reads=[K("zs"), "c"], writes=[bLk])
        LT = X["LT"]
        st.op("act", lambda e, bL=bL, LT=LT: e.copy(out=LT[:, 0:256], in_=bL[:, 0:256]), reads=[bLk], writes=[K("LT"), bLk])
        st.op("act", lambda e, bL=bL, LT=LT: e.copy(out=LT[0:32, 256:384], in_=bL[0:32, 256:384]), reads=[bLk], writes=[K("LT"), bLk])
        if STOP == 1:
            return
        bW, bWk = bank()
        bA, bAk = bank()
        st.op("pe", lambda e, bW=bW, LT=LT: e.matmul(bW[:], lhsT=LT[0:64, 0:128], rhs=Wwa[0:64, :], start=True, stop=True), reads=[K("LT"), "c"], writes=[bWk])
        st.op("pe", lambda e, bA=bA, LT=LT: e.matmul(bA[:], lhsT=LT[64:128, 0:128], rhs=Wwa[64:128, :], start=True, stop=True), reads=[K("LT"), "c"], writes=[bAk])
        sgw, a_, g_ = X["sgw"], X["a"], X["g"]
        st.op("dve", lambda e, bW=bW, sgw=sgw: e.tensor_tensor(out=sgw[:], in0=bW[:], in1=w0b[:], op=ALU.add), reads=[bWk, "c"], writes=[K("sgw"), bWk])
        st.op("act", lambda e, sgw=sgw: e.activation(out=sgw[:], in_=sgw[:], func=AF.Sigmoid), reads=[K("sgw")], writes=[K("sgw")])
        st.op("dve", lambda e, bA=bA, a_=a_: e.tensor_tensor(out=a_[:], in0=bA[:], in1=a0b[:], op=ALU.add), reads=[bAk, "c"], writes=[K("a"), bAk])
        st.op("act", lambda e, a_=a_: e.activation(out=a_[:], in_=a_[:], func=AF.Sigmoid), reads=[K("a")], writes=[K("a")])
        bG, bGk = bank()
        st.op("pe", lambda e, bG=bG, LT=LT: e.matmul(bG[:], lhsT=LT[:, 128:256], rhs=Wg[:, 0, :], start=True, stop=False), reads=[K("LT"), "c"], writes=[bGk])
        st.op("pe", lambda e, bG=bG, LT=LT: e.matmul(bG[:], lhsT=LT[0:32, 256:384], rhs=Wg[0:32, 1, :], start=False, stop=True), reads=[K("LT"), "c"], writes=[bGk])
        st.op("act", lambda e, bG=bG, g_=g_: e.copy(out=g_[:], in_=bG[:]), reads=[bGk], writes=[K("g"), bGk])
        if STOP == 2:
            return
        Ep, En, Ex, Et, gam = X["Ep"], X["En"], X["Ex"], X["Et"], X["gam"]
        bGm, bGmk = bank()
        for hp in range(4):
            st.op("pe", lambda e, bGm=bGm, sgw=sgw, hp=hp: e.matmul(bGm[:, hp:hp + 1], lhsT=sgw[:, hp * 128:(hp + 1) * 128], rhs=onesf[:, 0:1], start=True, stop=True), reads=[K("sgw"), "c"], writes=[bGmk])
        st.op("act", lambda e, bGm=bGm, gam=gam: e.activation(out=gam[:], in_=bGm[:, 0:4], func=AF.Exp, scale=DEC), reads=[bGmk], writes=[K("gam"), bGmk])
        bC, bCk = bank()
        bT_, bTk = bank()
        st.op("pe", lambda e, bC=bC, sgw=sgw: e.matmul(bC[:], lhsT=tri[:], rhs=sgw[:], start=True, stop=True), reads=[K("sgw"), "c"], writes=[bCk])
        st.op("pe", lambda e, bT_=bT_, sgw=sgw: e.matmul(bT_[:], lhsT=onesf[:], rhs=sgw[:], start=True, stop=True), reads=[K("sgw"), "c"], writes=[bTk])
        st.op("act", lambda e, bC=bC, Ep=Ep: e.activation(out=Ep[:], in_=bC[:], func=AF.Exp, scale=DEC), reads=[bCk], writes=[K("Ep"), bCk])
        st.op("act", lambda e, bC=bC, En=En: e.activation(out=En[:], in_=bC[:], func=AF.Exp, scale=-DEC), reads=[bCk], writes=[K("En"), bCk])
        st.op("act", lambda e, bT_=bT_, Et=Et: e.activation(out=Et[:], in_=bT_[:], func=AF.Exp, scale=DEC), reads=[bTk], writes=[K("Et"), bTk])
        st.op("act", lambda e, sgw=sgw, Ex=Ex: e.activation(out=Ex[:], in_=sgw[:], func=AF.Exp, scale=-DEC), reads=[K("sgw")], writes=[K("Ex")])
        st.op("pool", lambda e, Ex=Ex, Ep=Ep: e.tensor_tensor(out=Ex[:], in0=Ex[:], in1=Ep[:], op=ALU.mult), reads=[K("Ex"), K("Ep")], writes=[K("Ex")])
        st.op("pool", lambda e, Et=Et, En=En: e.tensor_tensor(out=Et[:], in0=Et[:], in1=En[:], op=ALU.mult), reads=[K("Et"), K("En")], writes=[K("Et")])
        if STOP == 3:
            return
        t0_, t1_, t2_, kk, k2, kka, sm = X["t0"], X["t1"], X["t2"], X["kk"], X["k2"], X["kka"], X["sm"]
        st.op("pool", lambda e, kk=kk, k=k: e.tensor_tensor(out=kk[:], in0=k, in1=kkb[:], op=ALU.mult), reads=[K("zs"), "c"], writes=[K("kk")])
        st.op("dve", lambda e, kk=kk, t0_=t0_: e.tensor_tensor(out=t0_[:], in0=kk[:], in1=kk[:], op=ALU.mult), reads=[K("kk")], writes=[K("t0")])
        st.op("dve", lambda e, t0_=t0_, sm=sm: e.tensor_reduce(out=sm[:, 0:8], in_=t0_[:].rearrange("p (h n) -> p h n", n=64), axis=AX.X, op=ALU.add), reads=[K("t0")], writes=[K("sm")])
        st.op("act", lambda e, sm=sm: e.sqrt(out=sm[:, 8:16], in_=sm[:, 0:8]), reads=[K("sm")], writes=[K("sm")])
        st.op("dve", lambda e, sm=sm: e.tensor_scalar_max(out=sm[:, 8:16], in0=sm[:, 8:16], scalar1=1e-12), reads=[K("sm")], writes=[K("sm")])
        st.op("dve", lambda e, sm=sm: e.reciprocal(out=sm[:, 8:16], in_=sm[:, 8:16]), reads=[K("sm")], writes=[K("sm")])
        st.op("dve", lambda e, kk=kk, sm=sm: e.tensor_tensor(out=kk[:].rearrange("p (h n) -> p h n", n=64), in0=kk[:].rearrange("p (h n) -> p h n", n=64),
                                                              in1=sm[:, 8:16].unsqueeze(2).to_broadcast([128, 8, 64]), op=ALU.mult), reads=[K("kk"), K("sm")], writes=[K("kk")])
        st.op("dve", lambda e, a_=a_, k2=k2: e.scalar_tensor_tensor(out=k2[:], in0=a_[:], scalar=-1.0, in1=kab[:], op0=ALU.add, op1=ALU.mult), reads=[K("a"), "c"], writes=[K("k2")])
        st.op("dve", lambda e, k2=k2, k=k: e.scalar_tensor_tensor(out=k2[:], in0=k2[:], scalar=1.0, in1=k, op0=ALU.add, op1=ALU.mult), reads=[K("k2"), K("zs")], writes=[K("k2")])
        st.op("pool", lambda e, kka=kka, kk=kk, a_=a_: e.tensor_tensor(out=kka[:], in0=kk[:], in1=a_[:], op=ALU.mult), reads=[K("kk"), K("a")], writes=[K("kka")])
        RT, AT, BT_, KT_, Bh, Kh, vb = X["RT"], X["AT"], X["BT_"], X["KT_"], X["Bh"], X["Kh"], X["vb"]
        st.op("pool", lambda e, RT=RT, r=r, Ep=Ep: e.tensor_tensor(out=RT[:], in0=r, in1=Ep[:], op=ALU.mult), reads=[K("zs"), K("Ep")], writes=[K("RT")])
        st.op("dve", lambda e, AT=AT, kk=kk, Ex=Ex: e.scalar_tensor_tensor(out=AT[:], in0=kk[:], scalar=-1.0, in1=Ex[:], op0=ALU.mult, op1=ALU.mult), reads=[K("kk"), K("Ex")], writes=[K("AT")])
        st.op("pool", lambda e, t1_=t1_, kka=kka, En=En: e.tensor_tensor(out=t1_[:], in0=kka[:], in1=En[:], op=ALU.mult), reads=[K("kka"), K("En")], writes=[K("t1")])
        st.op("dve", lambda e, t2_=t2_, k2=k2, En=En: e.tensor_tensor(out=t2_[:], in0=k2[:], in1=En[:], op=ALU.mult), reads=[K("k2"), K("En")], writes=[K("t2")])
        st.op("act", lambda e, BT_=BT_, t1_=t1_: e.copy(out=BT_[:], in_=t1_[:]), reads=[K("t1")], writes=[K("BT_")])
        st.op("act", lambda e, KT_=KT_, t2_=t2_: e.copy(out=KT_[:], in_=t2_[:]), reads=[K("t2")], writes=[K("KT_")])
        st.op("pool", lambda e, Bh=Bh, kka=kka, Et=Et: e.tensor_tensor(out=Bh[:], in0=kka[:], in1=Et[:], op=ALU.mult), reads=[K("kka"), K("Et")], writes=[K("Bh")])
        st.op("dve", lambda e, Kh=Kh, k2=k2, Et=Et: e.tensor_tensor(out=Kh[:], in0=k2[:], in1=Et[:], op=ALU.mult), reads=[K("k2"), K("Et")], writes=[K("Kh")])
        st.op("act", lambda e, vb=vb, v=v: e.copy(out=vb[:], in_=v), reads=[K("zs")], writes=[K("vb")])
        bon = X["bon"]
        st.op("pool", lambda e, bon=bon, r=r, k2=k2: e.tensor_tensor(out=bon[:], in0=r, in1=k2[:], op=ALU.mult), reads=[K("zs"), K("k2")], writes=[K("bon")])
        st.op("pool", lambda e, bon=bon: e.tensor_tensor(out=bon[:], in0=bon[:], in1=rkb[:], op=ALU.mult), reads=[K("bon"), "c"], writes=[K("bon")])
        st.op("dve", lambda e, bon=bon, sm=sm: e.tensor_reduce(out=sm[:, 16:24], in_=bon[:].rearrange("p (h n) -> p h n", n=64), axis=AX.X, op=ALU.add), reads=[K("bon")], writes=[K("sm")])
        st.op("dve", lambda e, bon=bon, sm=sm, v=v: e.tensor_tensor(out=bon[:].rearrange("p (h n) -> p h n", n=64), in0=v.rearrange("p (h n) -> p h n", n=64),
                                                                     in1=sm[:, 16:24].unsqueeze(2).to_broadcast([128, 8, 64]), op=ALU.mult), reads=[K("zs"), K("sm"), K("bon")], writes=[K("bon")])
        if STOP == 4:
            return
        ART, BTT, KTT = X["ART"], X["BTT"], X["KTT"]
        for ti, (src, sk, dst_fn, dk) in enumerate(((AT, K("AT"), lambda: ART[:, :, 0, :], K("ART")), (RT, K("RT"), lambda: ART[:, :, 1, :], K("ART")),
                                                    (BT_, K("BT_"), lambda: BTT[:, :, :], K("BTT")), (KT_, K("KT_"), lambda: KTT[:, :, :], K("KTT")))):
            bT2, hk = bank()
            PTv = bT2[:].bitcast(BF16)
            for hp in range(4):
                st.op("pe", lambda e, src=src, hp=hp, PTv=PTv: e.transpose(PTv[:, hp * 128:(hp + 1) * 128], src[:, hp * 128:(hp + 1) * 128], identb[:]),
                      reads=[sk, "c"], writes=[hk])
            eng = "act" if ti % 2 == 0 else "dve"
            if eng == "act":
                st.op("act", lambda e, dst_fn=dst_fn, PTv=PTv: e.copy(out=dst_fn(), in_=PTv[:, 0:512].rearrange("p (a b) -> p a b", a=4)), reads=[hk], writes=[dk, hk])
            else:
                st.op("dve", lambda e, dst_fn=dst_fn, PTv=PTv: e.tensor_copy(out=dst_fn(), in_=PTv[:, 0:512].rearrange("p (a b) -> p a b", a=4)), reads=[hk], writes=[dk, hk])
        if STOP == 5:
            return
        ABT, AKT = X["ABT"], X["AKT"]
        A_ = [X["A0"], X["A1"]]
        M_ = [X["M0"], X["M1"]]
        P_ = [X["P0"], X["P1"]]
        Ak = [K("A0"), K("A1")]
        Mk = [K("M0"), K("M1")]
        Pk = [K("P0"), K("P1")]
        for (lt_, lk_, dst, dkey) in ((BTT, K("BTT"), ABT, K("ABT")), (KTT, K("KTT"), AKT, K("AKT"))):
            for par in range(2):
                for q2 in range(2):
                    hA = 4 * q2 + par
                    b_, bk_ = bank()
                    b0 = 64 * par
                    for hh in range(2):
                        h = hA + 2 * hh
                        hp = h // 2
                        st.op("pe", lambda e, b_=b_, hh=hh, lt_=lt_, hp=hp, b0=b0: e.matmul(b_[:, hh * 256:(hh + 1) * 256], lhsT=lt_[b0:b0 + 64, hp, :], rhs=ART[b0:b0 + 64, hp, :, :].rearrange("p a b -> p (a b)"), start=True, stop=True),
                              reads=[lk_, K("ART")], writes=[bk_])
                    st.op("dve", lambda e, b_=b_, dst=dst, hA=hA: e.tensor_tensor(out=dst[:, hA:hA + 3:2, :, :], in0=b_[:].rearrange("p (a b c) -> p a b c", a=2, b=2), in1=mk12[:].rearrange("p (a b c) -> p a b c", a=2, b=2), op=ALU.mult),
                          reads=[bk_, "c"], writes=[dkey, bk_])
        for par in range(2):
            b_, bk_ = bank()
            b0 = 64 * par
            for hh in range(4):
                h = par + 2 * hh
                hp = h // 2
                st.op("pe", lambda e, b_=b_, hh=hh, hp=hp, b0=b0: e.matmul(b_[:, hh * 128:(hh + 1) * 128], lhsT=ART[b0:b0 + 64, hp, 0, :], rhs=BTT[b0:b0 + 64, hp, :], start=True, stop=True),
                      reads=[K("ART"), K("BTT")], writes=[bk_])
            st.op("dve", lambda e, b_=b_, par=par: e.tensor_tensor(out=A_[0][:, par:par + 7:2, :], in0=b_[:].rearrange("p (a b) -> p a b", a=4), in1=mk3[:].rearrange("p (a b) -> p a b", a=4), op=ALU.mult),
                  reads=[bk_, "c"], writes=[Ak[0], bk_])
        if STOP == 6:
            return
        cur_pool[0] = "i0"
        st.op("pool", lambda e: e.tensor_copy(out=M_[0][:], in_=ABT[:, :, 0, :]), reads=[K("ABT")], writes=[Mk[0]])
        st.op("pool", lambda e: e.tensor_tensor(out=P_[0][:], in0=ABT[:, :, 0, :], in1=identb[:].unsqueeze(1).to_broadcast([128, 8, 128]), op=ALU.add), reads=[K("ABT"), "c"], writes=[Pk[0]])
        cur = 0
        for lvl in range(1, 7):
            nxt = 1 - cur
            for hq in range(2):
                b_, bk_ = bank()
                for hh in range(4):
                    h = hq * 4 + hh
                    st.op("pe", lambda e, b_=b_, hh=hh, h=h, cur=cur: e.matmul(b_[:, hh * 128:(hh + 1) * 128], lhsT=M_[cur][:, h, :], rhs=A_[cur][:, h, :], start=True, stop=True),
                          reads=[Mk[cur], Ak[cur]], writes=[bk_])
                st.op("act", lambda e, b_=b_, hq=hq, nxt=nxt: e.copy(out=A_[nxt][:, hq * 4:hq * 4 + 4, :].rearrange("p a b -> p (a b)"), in_=b_[:]), reads=[bk_], writes=[Ak[nxt], bk_])
            if lvl < 6:
                for hq in range(2):
                    b_, bk_ = bank()
                    for hh in range(4):
                        h = hq * 4 + hh
                        st.op("pe", lambda e, b_=b_, hh=hh, h=h, cur=cur: e.matmul(b_[:, hh * 128:(hh + 1) * 128], lhsT=A_[cur][:, h, :], rhs=M_[cur][:, h, :], start=True, stop=True),
                              reads=[Mk[cur], Ak[cur]], writes=[bk_])
                    st.op("act", lambda e, b_=b_, hq=hq, nxt=nxt: e.copy(out=M_[nxt][:, hq * 4:hq * 4 + 4, :].rearrange("p a b -> p (a b)"), in_=b_[:]), reads=[bk_], writes=[Mk[nxt], bk_])
            for hq in range(2):
                b_, bk_ = bank()
                for hh in range(4):
                    h = hq * 4 + hh
                    st.op("pe", lambda e, b_=b_, hh=hh, h=h, cur=cur, nxt=nxt: e.matmul(b_[:, hh * 128:(hh + 1) * 128], lhsT=A_[nxt][:, h, :], rhs=P_[cur][:, h, :], start=True, stop=True),
                          reads=[Ak[nxt], Pk[cur]], writes=[bk_])
                st.op("dve", lambda e, b_=b_, hq=hq, cur=cur, nxt=nxt: e.tensor_tensor(out=P_[nxt][:, hq * 4:hq * 4 + 4, :].rearrange("p a b -> p (a b)"), in0=b_[:], in1=P_[cur][:, hq * 4:hq * 4 + 4, :].rearrange("p a b -> p (a b)"), op=ALU.add),
                      reads=[bk_, Pk[cur]], writes=[Pk[nxt], bk_])
            cur = nxt
        Pf, Pfk = P_[cur], Pk[cur]
        if STOP == 7:
            return
        cur_pool[0] = "s"
        Zs, Us = X["Zs"], X["Us"]
        bZ, bZk = bank()
        for h in range(8):
            hp, b0 = h // 2, 64 * (h % 2)
            st.op("pe", lambda e, bZ=bZ, h=h, hp=hp, b0=b0: e.matmul(bZ[:, h * 64:(h + 1) * 64], lhsT=ART[b0:b0 + 64, hp, 0, :], rhs=Hb[hp][b0:b0 + 64, :], start=True, stop=False),
                  reads=[K("ART"), ("Hb", hp)], writes=[bZk])
            st.op("pe", lambda e, bZ=bZ, h=h: e.matmul(bZ[:, h * 64:(h + 1) * 64], lhsT=AKT[:, h, 0, :], rhs=vb[:, h * 64:(h + 1) * 64], start=False, stop=True),
                  reads=[K("AKT"), K("vb")], writes=[bZk])
        st.op("act", lambda e, bZ=bZ, Zs=Zs: e.copy(out=Zs[:], in_=bZ[:]), reads=[bZk], writes=[K("Zs"), bZk])
        bU, bUk = bank()
        for h in range(8):
            st.op("pe", lambda e, bU=bU, h=h, Pf=Pf: e.matmul(bU[:, h * 64:(h + 1) * 64], lhsT=Pf[:, h, :], rhs=Zs[:, h * 64:(h + 1) * 64], start=True, stop=True),
                  reads=[Pfk, K("Zs")], writes=[bUk])
        st.op("dve", lambda e, bU=bU, Us=Us: e.tensor_copy(out=Us[:], in_=bU[:]), reads=[bUk], writes=[K("Us"), bUk])
        bY, bYk = bank()
        for h in range(8):
            hp, b0 = h // 2, 64 * (h % 2)
            st.op("pe", lambda e, bY=bY, h=h, hp=hp, b0=b0: e.matmul(bY[:, h * 64:(h + 1) * 64], lhsT=ART[b0:b0 + 64, hp, 1, :], rhs=Hb[hp][b0:b0 + 64, :], start=True, stop=False),
                  reads=[K("ART"), ("Hb", hp)], writes=[bYk])
            st.op("pe", lambda e, bY=bY, h=h: e.matmul(bY[:, h * 64:(h + 1) * 64], lhsT=ABT[:, h, 1, :], rhs=Us[:, h * 64:(h + 1) * 64], start=False, stop=False),
                  reads=[K("ABT"), K("Us")], writes=[bYk])
            st.op("pe", lambda e, bY=bY, h=h: e.matmul(bY[:, h * 64:(h + 1) * 64], lhsT=AKT[:, h, 1, :], rhs=vb[:, h * 64:(h + 1) * 64], start=False, stop=True),
                  reads=[K("AKT"), K("vb")], writes=[bYk])
        bH, bHk = bank()
        for hp in range(4):
            cs = slice(hp * 128, (hp + 1) * 128)
            st.op("pe", lambda e, bH=bH, cs=cs: e.matmul(bH[:, cs], lhsT=Bh[:, cs], rhs=Us[:, cs], start=True, stop=False), reads=[K("Bh"), K("Us")], writes=[bHk])
            st.op("pe", lambda e, bH=bH, cs=cs: e.matmul(bH[:, cs], lhsT=Kh[:, cs], rhs=vb[:, cs], start=False, stop=True), reads=[K("Kh"), K("vb")], writes=[bHk])
        Ysb = X["Ysb"]
        st.op("act", lambda e, bY=bY, Ysb=Ysb: e.copy(out=Ysb[:], in_=bY[:]), reads=[bYk], writes=[K("Ysb"), bYk])
        for hp in range(4):
            for hh in range(2):
                ps_ = slice(hh * 64, (hh + 1) * 64)
                st.op("dve", lambda e, hp=hp, hh=hh, ps_=ps_, bH=bH: e.scalar_tensor_tensor(out=H[hp][ps_, :], in0=H[hp][ps_, :], scalar=gam[ps_, hp:hp + 1], in1=bH[ps_, hp * 128 + hh * 64:hp * 128 + (hh + 1) * 64], op0=ALU.mult, op1=ALU.add),
                      reads=[("H", hp), K("gam"), bHk], writes=[("H", hp), bHk])
            st.op("pool", lambda e, hp=hp: e.tensor_copy(out=Hb[hp][:], in_=H[hp][:]), reads=[("H", hp)], writes=[("Hb", hp)])
        if STOP == 8:
            return
        Ysq, ya = X["Ysq"], X["ya"]
        v3 = lambda t: t[:].rearrange("p (h n) -> p h n", n=64)
        bc = lambda a: a.unsqueeze(2).to_broadcast([128, 8, 64])
        st.op("dve", lambda e, Ysb=Ysb, sm=sm: e.tensor_reduce(out=sm[:, 24:32], in_=v3(Ysb), axis=AX.X, op=ALU.add), reads=[K("Ysb")], writes=[K("sm")])
        st.op("pool", lambda e, Ysb=Ysb, Ysq=Ysq: e.tensor_tensor(out=Ysq[:], in0=Ysb[:], in1=Ysb[:], op=ALU.mult), reads=[K("Ysb")], writes=[K("Ysq")])
        st.op("dve", lambda e, Ysq=Ysq, sm=sm: e.tensor_reduce(out=sm[:, 32:40], in_=v3(Ysq), axis=AX.X, op=ALU.add), reads=[K("Ysq")], writes=[K("sm")])
        st.op("dve", lambda e, sm=sm: e.tensor_scalar_mul(out=sm[:, 40:48], in0=sm[:, 24:32], scalar1=1.0 / 64), reads=[K("sm")], writes=[K("sm")])
        st.op("dve", lambda e, sm=sm: e.tensor_tensor(out=sm[:, 48:56], in0=sm[:, 40:48], in1=sm[:, 40:48], op=ALU.mult), reads=[K("sm")], writes=[K("sm")])
        st.op("dve", lambda e, sm=sm: e.scalar_tensor_tensor(out=sm[:, 48:56], in0=sm[:, 32:40], scalar=1.0 / 64, in1=sm[:, 48:56], op0=ALU.mult, op1=ALU.subtract), reads=[K("sm")], writes=[K("sm")])
        st.op("dve", lambda e, sm=sm: e.tensor_scalar_add(out=sm[:, 48:56], in0=sm[:, 48:56], scalar1=GN_EPS), reads=[K("sm")], writes=[K("sm")])
        st.op("act", lambda e, sm=sm: e.sqrt(out=sm[:, 48:56], in_=sm[:, 48:56]), reads=[K("sm")], writes=[K("sm")])
        st.op("dve", lambda e, sm=sm: e.reciprocal(out=sm[:, 48:56], in_=sm[:, 48:56]), reads=[K("sm")], writes=[K("sm")])
        st.op("dve", lambda e, Ysb=Ysb, ya=ya, sm=sm: e.tensor_tensor(out=v3(ya), in0=v3(Ysb), in1=bc(sm[:, 40:48]), op=ALU.subtract), reads=[K("Ysb"), K("sm")], writes=[K("ya")])
        st.op("dve", lambda e, ya=ya, sm=sm: e.tensor_tensor(out=v3(ya), in0=v3(ya), in1=bc(sm[:, 48:56]), op=ALU.mult), reads=[K("ya"), K("sm")], writes=[K("ya")])
        st.op("pool", lambda e, ya=ya: e.tensor_tensor(out=ya[:], in0=ya[:], in1=lgb[:], op=ALU.mult), reads=[K("ya"), "c"], writes=[K("ya")])
        st.op("pool", lambda e, ya=ya: e.tensor_tensor(out=ya[:], in0=ya[:], in1=lbb[:], op=ALU.add), reads=[K("ya"), "c"], writes=[K("ya")])
        st.op("pool", lambda e, ya=ya, bon=bon: e.tensor_tensor(out=ya[:], in0=ya[:], in1=bon[:], op=ALU.add), reads=[K("ya"), K("bon")], writes=[K("ya")])
        st.op("pool", lambda e, ya=ya, g_=g_: e.tensor_tensor(out=ya[:], in0=ya[:], in1=g_[:], op=ALU.mult), reads=[K("ya"), K("g")], writes=[K("ya")])
        bO, bOk = bank()
        for q in range(4):
            st.op("pe", lambda e, bO=bO, q=q, ya=ya: e.transpose(bO[:, q * 128:(q + 1) * 128], ya[:, q * 128:(q + 1) * 128], identf[:]), reads=[K("ya"), "c"], writes=[bOk])
        q4 = (c // 4) % 2
        c4 = c % 4
        yk = ("yaT4", q4)
        st.op("act", lambda e, bO=bO, q4=q4, c4=c4: e.copy(out=yaT4[q4][:, :, c4 * 128:(c4 + 1) * 128], in_=bO[:].rearrange("p (a b) -> p a b", a=4)), reads=[bOk], writes=[yk, bOk])
        if c4 == 3 or c == nch - 1:
            tb = (c - c4) * 128
            w_ = (c4 + 1) * 128
            st.dma("sync", yT_d[0:512, tb:tb + w_].rearrange("(kc p) t -> p kc t", p=128), yaT4[q4][:, :, 0:w_], reads=[yk], key=("yo", q4))

    for c in range(nch):
        do_chunk(c)
    if finish:
        st.finish()

def rope_tables():
    half = 16
    inv_freq = (np.float32(500000.0) ** (-(np.arange(half, dtype=np.float32)) / np.float32(half))).astype(np.float32)
    ang = (np.arange(T, dtype=np.float32)[:, None] * inv_freq[None, :]).astype(np.float32)
    c = np.cos(ang.astype(np.float64)).astype(np.float32).T
    s = np.sin(ang.astype(np.float64)).astype(np.float32).T
    cos = np.concatenate([c, c], 0)
    sin = np.concatenate([-s, s], 0)
    return np.ascontiguousarray(cos), np.ascontiguousarray(sin)
def pswap():
    P = np.zeros((128, 128), np.float32)
    for m in range(32):
        P[(m + 16) % 32, m] = 1.0
    return P
def dil_mask():
    k = np.arange(128)[:, None]; q = np.arange(128)[None, :]
    A = np.where(k >= q, 0.0, NEG); B = np.where(k <= q, 0.0, NEG)
    return np.ascontiguousarray(np.concatenate([A, B], 1).astype(np.float32))
def moba_cmask():
    k = np.arange(128)[:, None]; q = np.arange(256)[None, :]
    m = np.stack([np.where(128 * c + k <= q, 0.0, NEG) for c in range(2)], 1)
    return np.ascontiguousarray(m.astype(np.float32))
def selall():
    s = np.zeros((16, 16 * 128), np.float32)
    for j in range(16): s[j, j * 128:(j + 1) * 128] = 1.0
    return s
def rwkv_consts():
    s = np.arange(128)[:, None]; t = np.arange(128)[None, :]
    tri = (s <= t).astype(np.float32)
    lt = (s < t).astype(np.float32); le = (s <= t).astype(np.float32)
    mk12 = np.concatenate([lt, le, lt, le], 1)
    gt = (t < s).astype(np.float32)
    mk3 = np.concatenate([gt] * 4, 1)
    return tri, np.ascontiguousarray(mk12), np.ascontiguousarray(mk3)


def _build_program():
    nc = bass.Bass("TRN2", target_bir_lowering=False)
    prog = Prog(nc)

    def din(name, shape, dt=F32):
        return nc.dram_tensor(name, list(shape), dt, kind="ExternalInput").ap()

    def scr(name, shape, dt):
        return nc.dram_tensor(name, list(shape), dt, kind="Internal").ap()

    x = din("x", [T, D])
    ab_w_in = din("ab_w_in", [D, 3360])
    prm = {"shift_mix": din("ab_shift_mix", [1, 1824]), "w0": din("ab_w0", [1, 512]), "w_up": din("ab_w_up", [64, 512]),
           "a0": din("ab_a0", [1, 512]), "a_up": din("ab_a_up", [64, 512]), "g_up": din("ab_g_up", [160, 512]),
           "k_k": din("ab_k_k", [1, 512]), "k_a": din("ab_k_a", [1, 512]), "r_k": din("ab_r_k", [1, 512]),
           "lnx_g": din("ab_lnx_g", [1, 512]), "lnx_b": din("ab_lnx_b", [1, 512])}
    ab_w_out = din("ab_w_out", [D, D])
    c_w_in = din("c_w_in", [D, 5120])
    c_w_out = din("c_w_out", [D, D])
    ln1_g = din("ln1_g", [2, D]); ln1_b = din("ln1_b", [2, D]); ln2_g = din("ln2_g", [2, D]); ln2_b = din("ln2_b", [2, D])
    mlp_w1 = din("mlp_w1", [2, D, 4 * D]); mlp_w2 = din("mlp_w2", [2, 4 * D, D])
    cd = {"ident": din("c_ident", [128, 128]), "ones": din("c_ones", [128, 128]), "cos": din("c_cos", [32, T]), "sin": din("c_sin", [32, T]),
          "pswap": din("c_pswap", [128, 128]), "dmask": din("c_dmask", [128, 256]), "cmask": din("c_cmask", [128, 2, 256]),
          "selall": din("c_selall", [16, 2048]), "tri": din("c_tri", [128, 128]), "mk12": din("c_mk12", [128, 512]), "mk3": din("c_mk3", [128, 512])}
    out = nc.dram_tensor("out", [T, D], F32, kind="ExternalOutput").ap()

    zr = scr("s_zr", [T, 1824], F32)
    mq = scr("s_mq", [4, 128, T], BF16); mk = scr("s_mk", [4, 128, T], BF16); mv = scr("s_mv", [T, 512], BF16)
    yT = scr("s_yT", [D, T], BF16)
    x1 = scr("s_x1", [T, D], F32); x1T = scr("s_x1T", [D, T], BF16)
    xmid = scr("s_xmid", [T, D], F32)
    dq = scr("s_dq", [24, 128, T], BF16); dk = scr("s_dk", [8, 128, T], BF16); dv = scr("s_dv", [8, T, 128], BF16)
    oT = scr("s_oT", [D, T], BF16)

    groups = []
    for h in range(4):
        groups.append({"kind": "fm_rot", "col": 1824 + h * 128, "dst": mq[h]})
        groups.append({"kind": "fm_rot", "col": 1824 + 512 + h * 128, "dst": mk[h]})
    groups.append({"kind": "tm", "col": 1824 + 1024, "n": 512, "dst": mv, "dt": BF16})
    for c0, n in ((0, 512), (512, 512), (1024, 512), (1536, 288)):
        groups.append({"kind": "tm", "col": c0, "n": n, "dst": zr, "dcol": c0, "dt": F32})
    stage_proj(prog, "p0", x, ab_w_in, 0, 3360, groups, cd)
    stage_rwkv(prog, "rw", zr, prm, yT, cd)
    stage_moba(prog, "mb", mq, mk, mv, yT, cd)
    stage_postA(prog, "a0", yT, x, ab_w_out, ln1_g[0:1, :], ln1_b[0:1, :], x1, x1T, cd)
    stage_mlp(prog, "m0", x1, x1T, mlp_w1[0], mlp_w2[0], ln2_g[0:1, :], ln2_b[0:1, :], xmid)
    groups = [{"kind": "fm_rot", "col": i * 128, "dst": dq[i]} for i in range(20)]
    stage_proj(prog, "p1", xmid, c_w_in, 0, 2560, groups, cd)
    groups = [{"kind": "fm_rot", "col": (i - 20) * 128, "dst": dq[i]} for i in range(20, 24)]
    groups += [{"kind": "fm_rot", "col": 512 + i * 128, "dst": dk[i]} for i in range(8)]
    def _vdst(c):
        return lambda ta, stg: (dv[4 * c:4 * c + 4, ta:ta + 128, :].rearrange("h t c -> t h c"), stg[:, 0:512].rearrange("p (h c) -> p h c", h=4))
    groups += [{"kind": "tm", "col": 1536 + c * 512, "n": 512, "dst": dv, "dst_fn": _vdst(c), "dt": BF16} for c in range(2)]
    stage_proj(prog, "p2", xmid, c_w_in, 2560, 2560, groups, cd)
    stage_dil(prog, "dl", dq, dk, dv, oT, cd)
    stage_postA(prog, "a1", oT, xmid, c_w_out, ln1_g[1:2, :], ln1_b[1:2, :], x1, x1T, cd)
    stage_mlp(prog, "m1", x1, x1T, mlp_w1[1], mlp_w2[1], ln2_g[1:2, :], ln2_b[1:2, :], out)
    prog.close()
    return nc


def kernel(**inputs):
    f = lambda a: np.ascontiguousarray(np.asarray(a, dtype=np.float32))
    cos, sin = rope_tables()
    tri, mk12, mk3 = rwkv_consts()
    shared = {
        "ab_w_in": f(inputs["ab_w_in"][0]), "ab_shift_mix": f(inputs["ab_shift_mix"]).reshape(1, 1824), "ab_w0": f(inputs["ab_w0"]).reshape(1, 512),
        "ab_w_up": f(inputs["ab_w_up"][0]), "ab_a0": f(inputs["ab_a0"]).reshape(1, 512), "ab_a_up": f(inputs["ab_a_up"][0]),
        "ab_g_up": f(inputs["ab_g_up"][0]), "ab_k_k": f(inputs["ab_k_k"]).reshape(1, 512), "ab_k_a": f(inputs["ab_k_a"]).reshape(1, 512),
        "ab_r_k": f(inputs["ab_r_k"]).reshape(1, 512), "ab_lnx_g": f(inputs["ab_lnx_g"]).reshape(1, 512), "ab_lnx_b": f(inputs["ab_lnx_b"]).reshape(1, 512),
        "ab_w_out": f(inputs["ab_w_out"][0]), "c_w_in": f(inputs["c_w_in"][0]), "c_w_out": f(inputs["c_w_out"][0]),
        "ln1_g": f(inputs["ln1_g"]), "ln1_b": f(inputs["ln1_b"]), "ln2_g": f(inputs["ln2_g"]), "ln2_b": f(inputs["ln2_b"]),
        "mlp_w1": f(inputs["mlp_w1"]), "mlp_w2": f(inputs["mlp_w2"]),
        "c_ident": np.eye(128, dtype=np.float32), "c_ones": np.ones((128, 128), np.float32), "c_cos": cos, "c_sin": sin, "c_pswap": pswap(),
        "c_dmask": dil_mask(), "c_cmask": moba_cmask(), "c_selall": selall(), "c_tri": tri, "c_mk12": mk12, "c_mk3": mk3,
    }
    xs = f(inputs["x"])
    nc = _build_program()
    in_maps = [dict(shared, x=np.ascontiguousarray(xs[b])) for b in range(8)]
    res = run_bass_kernel_spmd(nc, in_maps, core_ids=list(range(8)))
    return np.stack([np.asarray(r["out"], dtype=np.float32) for r in res.results], 0)
```

```python
import numpy as np
from contextlib import ExitStack
import concourse.bass as bass
import concourse.mybir as mybir
from concourse.bass_utils import run_bass_kernel_spmd

F32 = mybir.dt.float32
BF16 = mybir.dt.bfloat16
AF = mybir.ActivationFunctionType
ALU = mybir.AluOpType
AX = mybir.AxisListType

ENGS = ("sync", "act", "dve", "pool", "pe")
DEFAULT_COST = {"sync": 0.05, "act": 0.5, "dve": 0.55, "pool": 0.7, "pe": 0.12, "dma": 3.0}
SEM_LAT = 0.4
SAME_LAT = 0.05
SCHEDULE = True
CRITPATH = False


class _Op:
    __slots__ = ("eng", "fn", "deps", "odeps", "dma_key", "dma_cnt", "sig", "cnt", "cost", "idx", "succ", "nd", "rt", "tag", "ef", "cp")


class Prog:
    def __init__(self, nc):
        self.nc = nc
        self.es = ExitStack()
        self.engsem = {e: self.es.enter_context(nc.semaphore(f"s_{e}")) for e in ENGS}
        self.engcnt = {e: 0 for e in ENGS}
        self.dma_pool = []
        self.dma_free = {}
        self.nstage = 0

    def get_dma_sem(self, cls):
        fl = self.dma_free.setdefault(cls, [])
        if fl:
            return fl.pop()
        h = self.es.enter_context(self.nc.semaphore(f"s_dma{len(self.dma_pool)}"))
        ent = [h, 0, cls]
        self.dma_pool.append(ent)
        return ent

    def close(self):
        self.es.close()


class Stage:
    def __init__(self, prog, name):
        self.prog = prog
        self.nc = prog.nc
        self.name = name
        self.ops = {e: [] for e in ENGS}
        self.lastw = {}
        self.readers = {}
        self.dma_ents = {}
        self.dma_base = {}
        self.dma_counts = {}
        self.order = []
        self.last_dma = {}
        self.ns = None
        self.costs = {}
        self.prio_cp = False
        self.es = ExitStack()

    def sb(self, name, shape, dt):
        return self.es.enter_context(self.nc.sbuf_tensor(f"{self.name}_{self.ns or ""}{name}", list(shape), dt))

    def ps(self, name, shape, dt):
        return self.es.enter_context(self.nc.psum_tensor(f"{self.name}_{self.ns or ""}{name}", list(shape), dt))

    def _mk(self, eng, fn, reads, writes, dma_key=None, cost=None):
        if self.ns is not None:
            reads = [(self.ns, r) for r in reads]
            writes = [(self.ns, w) for w in writes]
            if dma_key is not None:
                dma_key = (self.ns, dma_key)
        op = _Op()
        op.eng = eng
        op.fn = fn
        op.dma_key = dma_key
        op.sig = False
        op.cnt = None
        op.dma_cnt = None
        deps = {}
        for r in reads:
            w = self.lastw.get(r)
            if w is not None:
                deps[id(w)] = w
        for w_ in writes:
            w = self.lastw.get(w_)
            if w is not None:
                deps[id(w)] = w
            for rd in self.readers.get(w_, ()):
                deps[id(rd)] = rd
        dl = []
        ol = []
        for d in deps.values():
            if d is op:
                continue
            if d.dma_key is None and d.eng == "pe" and eng == "pe" and dma_key is None:
                ol.append(d)
                continue
            if d.dma_key is not None:
                dl.append((d, self.dma_ents[d.dma_key][1]))
                ld = self.last_dma.get(d.dma_key)
                if ld is not None and ld is not d:
                    ol.append(ld)
            else:
                dl.append((d, None))
                d.sig = True
        op.deps = dl
        op.odeps = ol
        op.cost = cost if cost is not None else (self.costs.get(eng, DEFAULT_COST[eng]) if dma_key is None else DEFAULT_COST["dma"])
        op.idx = len(self.order)
        import sys as _s
        op.tag = _s._getframe(2).f_lineno
        self.order.append(op)
        if dma_key is not None:
            prev = self.last_dma.get(dma_key)
            if prev is not None:
                ol.append(prev)
            self.last_dma[dma_key] = op
        rk = dma_key if dma_key is not None else eng
        for r in reads:
            self.readers.setdefault(r, []).append(op)
        for w_ in writes:
            self.lastw[w_] = op
            self.readers[w_] = []
        if dma_key is not None:
            if dma_key not in self.dma_ents:
                ent = self.prog.get_dma_sem("sw" if eng == "pool" else "hw")
                self.dma_ents[dma_key] = ent
            ent = self.dma_ents[dma_key]
            assert ent[2] == ("sw" if eng == "pool" else "hw"), dma_key
            ent[1] += 1
            op.dma_cnt = ent[1]
        self.ops[eng].append(op)
        return op

    def op(self, eng, fn, reads=(), writes=(), c=None):
        return self._mk(eng, fn, reads, writes, cost=c)

    def dma(self, eng, out, in_, reads=(), writes=(), key=None, **kw):
        assert key is not None
        return self._mk(eng, lambda e: e.dma_start(out=out, in_=in_, **kw), reads, writes, dma_key=key)

    def schedule(self):
        import heapq
        ops = self.order
        for op in ops:
            op.succ = []
            op.rt = 0.0
        for op in ops:
            ds = {id(d): (d, False) for d, _ in op.deps}
            for d in op.odeps:
                if id(d) not in ds:
                    ds[id(d)] = (d, True)
            op.nd = len(ds)
            for d, oo in ds.values():
                d.succ.append((op, oo))
        if self.prio_cp:
            for op in reversed(ops):
                b = 0.0
                for s_, oo in op.succ:
                    l_ = 0.0 if oo else (SAME_LAT if (s_.eng == op.eng and op.dma_key is None) else SEM_LAT)
                    if s_.ef + l_ > b:
                        b = s_.ef + l_
                op.ef = b + op.cost
            for op in ops:
                op.idx = (-op.ef, op.idx)
        pend = {e: [] for e in ENGS}
        avail = {e: [] for e in ENGS}
        free = {e: 0.0 for e in ENGS}
        fin = {}
        for op in ops:
            if op.nd == 0:
                heapq.heappush(pend[op.eng], (0.0, op.idx, op))
        sched = {e: [] for e in ENGS}
        n = 0
        while n < len(ops):
            best = None
            for e in ENGS:
                pe_, av = pend[e], avail[e]
                while pe_ and pe_[0][0] <= free[e]:
                    _, i_, o_ = heapq.heappop(pe_)
                    heapq.heappush(av, (i_, o_))
                if av:
                    cand = (free[e], av[0][0], e, 0)
                elif pe_:
                    cand = (pe_[0][0], pe_[0][1], e, 1)
                else:
                    continue
                if best is None or cand < best:
                    best = cand
            start, _, e, src = best
            if src == 0:
                _, op = heapq.heappop(avail[e])
            else:
                _, _, op = heapq.heappop(pend[e])
            if op.dma_key is not None:
                free[e] = start + 0.06
                f_ = start + op.cost
            else:
                free[e] = start + op.cost
                f_ = start + op.cost
            sched[e].append(op)
            n += 1
            for s_, oo in op.succ:
                if oo:
                    t_ = free[e] if op.dma_key is not None else f_
                else:
                    t_ = f_ + (SAME_LAT if (s_.eng == op.eng and op.dma_key is None) else SEM_LAT)
                if t_ > s_.rt:
                    s_.rt = t_
                s_.nd -= 1
                if s_.nd == 0:
                    heapq.heappush(pend[s_.eng], (s_.rt, s_.idx, s_))
        self.ops = sched
        self.est = max(free.values())
        if CRITPATH:
            for op in ops:
                best, bp = 0.0, None
                ds = [d for d, _ in op.deps] + list(op.odeps)
                for d in ds:
                    lat = 0.05 if (d.eng == op.eng and d.dma_key is None) else SEM_LAT
                    if d.ef + lat > best:
                        best, bp = d.ef + lat, d
                op.ef = best + op.cost
                op.cp = bp
            last = max(ops, key=lambda o: o.ef)
            print(f"[critpath {self.name}] dependency-only critical path = {last.ef:.0f} us")
            path = []
            o = last
            while o is not None:
                path.append(o)
                o = o.cp
            path.reverse()
            import collections
            cnt = collections.Counter((o.eng, o.tag) for o in path)
            print("   top path contributors (eng, line, count):", cnt.most_common(25))

    def finish(self):
        prog = self.prog
        if SCHEDULE:
            self.schedule()
        for e in ENGS:
            c = prog.engcnt[e]
            for op in self.ops[e]:
                if op.dma_key is None and op.sig:
                    c += 1
                    op.cnt = c
            prog.engcnt[e] = c
        stage = self

        def emit(eng_name, e):
            waited = {}
            for op in stage.ops[eng_name]:
                for d, dv in op.deps:
                    if d.dma_key is not None:
                        sem = stage.dma_ents[d.dma_key][0]
                        val = 16 * dv
                    else:
                        sem = prog.engsem[d.eng]
                        val = d.cnt
                    k = id(sem)
                    if waited.get(k, 0) < val:
                        e.wait_ge(sem, val)
                        waited[k] = val
                inst = op.fn(e)
                if op.dma_key is not None:
                    inst.then_inc(stage.dma_ents[op.dma_key][0], 16)
                elif op.sig:
                    inst.then_inc(prog.engsem[eng_name], 1)
            finals = {}
            for op in stage.ops[eng_name]:
                if op.dma_key is not None:
                    ent = stage.dma_ents[op.dma_key]
                    finals[id(ent[0])] = (ent[0], 16 * ent[1])
            for sem, val in finals.values():
                if waited.get(id(sem), 0) < val:
                    e.wait_ge(sem, val)

        with self.nc.Block() as block:
            @block.sync
            def _(e):
                emit("sync", e)

            @block.scalar
            def _(e):
                emit("act", e)

            @block.vector
            def _(e):
                emit("dve", e)

            @block.gpsimd
            def _(e):
                emit("pool", e)

            @block.tensor
            def _(e):
                emit("pe", e)
        for ent in self.dma_ents.values():
            prog.dma_free.setdefault(ent[2], []).append(ent)
        self.es.close()
        n = {e: len(self.ops[e]) for e in ENGS}
        print(f"[stage {self.name}] ops: {n} est_us={getattr(self, 'est', 0):.0f}")


T = 4096
D = 1024
NT = T // 128
ALPHA = 4 ** 0.25
LN_EPS = 1e-5
NEG = -30000.0


def load_bcast(st, eng, dst, src_row_ap, n, key, res):
    st.dma(eng, dst, src_row_ap.broadcast_to([128, n]), writes=[res], key=key)


def stage_proj(prog, name, x_d, w_d, col_lo, ncols, groups, consts):
    st = Stage(prog, name)
    nc = st.nc
    xT = st.sb("xT", [128, 8, T], BF16)
    wb = st.sb("wb", [128, 8, ncols], BF16)
    idf = st.sb("idf", [128, 128], F32)
    cs_ = [st.sb(f"cs{i}", [32, 2, 512], F32) for i in range(2)]
    psw = st.sb("psw", [128, 128], BF16)
    xin = [st.sb(f"xin{i}", [128, D], F32) for i in range(2)]
    pA = [st.ps(f"pA{i}", [128, 512], F32) for i in range(2)]
    pB = [st.ps(f"pB{i}", [128, 512], F32) for i in range(2)]
    pC = [st.ps(f"pC{i}", [128, 512], F32) for i in range(2)]
    NQ = 3
    qt = [st.sb(f"qt{i}", [128, 512], BF16) for i in range(NQ)]
    t1 = [st.sb(f"t1{i}", [32, 512], F32) for i in range(2)]
    t2 = [st.sb(f"t2{i}", [32, 512], F32) for i in range(2)]
    NS = 3
    tms_f = [st.sb(f"tmsf{i}", [128, 512], F32) for i in range(NS)]
    tms_b = [st.sb(f"tmsb{i}", [128, 512], BF16) for i in range(NS)]

    st.dma("sync", idf[:], consts["ident"][:, :], writes=["idf"], key="c0")
    st.dma("pool", psw[:], consts["pswap"][:, :], writes=["psw"], key="c1")
    wv = w_d.rearrange("(kc p) n -> p kc n", p=128)
    for kc in range(8):
        st.dma("pool", wb[:, kc, :], wv[:, kc, col_lo:col_lo + ncols], writes=[("wb", kc)], key="wb")

    for ti in range(NT):
        s = ti % 2
        st.dma("sync", xin[s][:], x_d[ti * 128:(ti + 1) * 128, :], writes=[("xin", s)], key=("xin", s))
        for half in range(2):
            p = pA[(2 * ti + half) % 2]
            pk = ("pA", (2 * ti + half) % 2)
            for q in range(4):
                kc = half * 4 + q
                st.op("pe", lambda e, p=p, q=q, s=s, kc=kc: e.transpose(p[:, q * 128:(q + 1) * 128], xin[s][:, kc * 128:(kc + 1) * 128], idf[:]),
                      reads=[("xin", s), "idf"], writes=[pk])
            eng = "act" if half == 0 else "dve"
            if eng == "act":
                st.op("act", lambda e, p=p, half=half, ti=ti: e.copy(out=xT[:, half * 4:half * 4 + 4, ti * 128:(ti + 1) * 128], in_=p[:].rearrange("p (a b) -> p a b", a=4)),
                      reads=[pk], writes=[("xT", ti)])
            else:
                st.op("dve", lambda e, p=p, half=half, ti=ti: e.tensor_copy(out=xT[:, half * 4:half * 4 + 4, ti * 128:(ti + 1) * 128], in_=p[:].rearrange("p (a b) -> p a b", a=4)),
                      reads=[pk], writes=[("xT", ti)])

    fmi = 0
    tmi = 0
    rci = 0
    for tt in range(T // 512):
        t0 = tt * 512
        xres = [("xT", 4 * tt + i) for i in range(4)]
        csl = tt % 2
        if any(g["kind"] == "fm_rot" for g in groups):
            st.dma("sync", cs_[csl][:, 0, :], consts["cos"][:, t0:t0 + 512], writes=[("cs", csl)], key=("cs", csl))
            st.dma("sync", cs_[csl][:, 1, :], consts["sin"][:, t0:t0 + 512], writes=[("cs", csl)], key=("cs", csl))
        for g in groups:
            if g["kind"] in ("fm", "fm_rot"):
                c0 = g["col"]
                p = pA[fmi % 2]
                pk = ("pA", fmi % 2)
                for kc in range(8):
                    st.op("pe", lambda e, p=p, kc=kc, c0=c0, t0=t0: e.matmul(p[:], lhsT=wb[:, kc, c0:c0 + 128], rhs=xT[:, kc, t0:t0 + 512], start=(kc == 0), stop=(kc == 7)),
                          reads=xres + [("wb", kc)], writes=[pk])
                qs = fmi % NQ
                qk = ("qt", qs)
                q_ = qt[qs]
                st.op("act", lambda e, p=p, q_=q_: e.copy(out=q_[:], in_=p[:]), reads=[pk], writes=[qk, pk])
                if g["kind"] == "fm_rot":
                    pb = pB[rci % 2]
                    pbk = ("pB", rci % 2)
                    a1 = t1[rci % 2]
                    a2 = t2[rci % 2]
                    a1k = ("t1", rci % 2)
                    a2k = ("t2", rci % 2)
                    st.op("pe", lambda e, pb=pb, q_=q_: e.matmul(pb[:, :], lhsT=psw[:, :], rhs=q_[:, :], start=True, stop=True),
                          reads=[qk, "psw"], writes=[pbk])
                    st.op("dve", lambda e, p=p, a1=a1, csl=csl: e.tensor_tensor(out=a1[:], in0=p[0:32, :], in1=cs_[csl][:, 0, :], op=ALU.mult),
                          reads=[pk, ("cs", csl)], writes=[a1k, pk])
                    st.op("dve", lambda e, pb=pb, a2=a2, csl=csl: e.tensor_tensor(out=a2[:], in0=pb[0:32, :], in1=cs_[csl][:, 1, :], op=ALU.mult),
                          reads=[pbk, ("cs", csl)], writes=[a2k])
                    st.op("pool", lambda e, q_=q_, a1=a1, a2=a2: e.tensor_tensor(out=q_[0:32, :], in0=a1[:], in1=a2[:], op=ALU.add),
                          reads=[a1k, a2k, qk], writes=[qk])
                    rci += 1
                st.dma("sync", g["dst"][:, t0:t0 + 512], q_[:], reads=[qk], key=("qo", qs))
                fmi += 1
            else:
                c0 = g["col"]
                n = g["n"]
                for sub in range(4):
                    ta = t0 + sub * 128
                    p = pC[tmi % 2]
                    pk = ("pC", tmi % 2)
                    for kc in range(8):
                        st.op("pe", lambda e, p=p, kc=kc, c0=c0, n=n, ta=ta: e.matmul(p[:, 0:n], lhsT=xT[:, kc, ta:ta + 128], rhs=wb[:, kc, c0:c0 + n], start=(kc == 0), stop=(kc == 7)),
                              reads=[("xT", ta // 128), ("wb", kc)], writes=[pk])
                    ss = tmi % NS
                    stg = tms_f[ss] if g["dt"] == F32 else tms_b[ss]
                    sk = ("tms", ss)
                    eng = "act" if tmi % 2 == 0 else "dve"
                    if eng == "act":
                        st.op("act", lambda e, p=p, stg=stg, n=n: e.copy(out=stg[:, 0:n], in_=p[:, 0:n]), reads=[pk], writes=[sk])
                    else:
                        st.op("dve", lambda e, p=p, stg=stg, n=n: e.tensor_copy(out=stg[:, 0:n], in_=p[:, 0:n]), reads=[pk], writes=[sk])
                    if "dst_fn" in g:
                        o_ap, i_ap = g["dst_fn"](ta, stg)
                        st.dma("sync", o_ap, i_ap, reads=[sk], key=("to", ss))
                    else:
                        st.dma("sync", g["dst"][ta:ta + 128, g.get("dcol", 0):g.get("dcol", 0) + n], stg[:, 0:n], reads=[sk], key=("to", ss))
                    tmi += 1
    st.finish()


def layer_norm_tile(st, u, uk, outt, outk, gam, bet, tag, slot):
    stats, mv, rstd = st.lnbuf[slot]
    sk = ("lnstat", slot)
    for c in range(2):
        st.op("dve", lambda e, c=c: e.bn_stats(out=stats[:, c * 6:(c + 1) * 6], in_=u[:, c * 512:(c + 1) * 512]), reads=[uk], writes=[sk])
    st.op("dve", lambda e: e.bn_aggr(out=mv[:], in_=stats[:]), reads=[sk], writes=[sk])
    st.op("dve", lambda e: e.tensor_scalar_add(out=rstd[:], in0=mv[:, 1:2], scalar1=LN_EPS), reads=[sk], writes=[sk])
    st.op("act", lambda e: e.sqrt(out=rstd[:], in_=rstd[:]), reads=[sk], writes=[sk])
    st.op("dve", lambda e: e.reciprocal(out=rstd[:], in_=rstd[:]), reads=[sk], writes=[sk])
    st.op("dve", lambda e: e.tensor_scalar(out=outt[:], in0=u[:], scalar1=mv[:, 0:1], scalar2=rstd[:, 0:1], op0=ALU.subtract, op1=ALU.mult),
          reads=[uk, sk], writes=[outk])
    st.op("pool", lambda e: e.tensor_tensor(out=outt[:], in0=outt[:], in1=gam[:], op=ALU.mult), reads=[outk, "lnc"], writes=[outk])
    st.op("pool", lambda e: e.tensor_tensor(out=outt[:], in0=outt[:], in1=bet[:], op=ALU.add), reads=[outk, "lnc"], writes=[outk])


def stage_postA(prog, name, yT_d, xres_d, wout_d, g1_d, b1_d, x1_d, x1T_d, consts):
    st = Stage(prog, name)
    wo = st.sb("wo", [128, 8, D], BF16)
    idf = st.sb("idf", [128, 128], F32)
    g1 = st.sb("g1", [128, D], F32)
    b1 = st.sb("b1", [128, D], F32)
    yt = [st.sb(f"yt{i}", [128, 8, 512], BF16) for i in range(2)]
    xr = [st.sb(f"xr{i}", [128, D], F32) for i in range(2)]
    u = [st.sb(f"u{i}", [128, D], F32) for i in range(3)]
    xT = [st.sb(f"xT{i}", [128, 8, 512], BF16) for i in range(2)]
    st.lnbuf = [(st.sb(f"lns{i}", [128, 12], F32), st.sb(f"lnm{i}", [128, 2], F32), st.sb(f"lnr{i}", [128, 1], F32)) for i in range(3)]
    ph = [st.ps(f"ph{i}", [128, 512], F32) for i in range(4)]
    pt = [st.ps(f"pt{i}", [128, 512], F32) for i in range(2)]
    st.dma("sync", idf[:], consts["ident"][:, :], writes=["idf"], key="c0")
    for (dst, src) in ((g1, g1_d), (b1, b1_d)):
        st.dma("sync", dst[:], src.broadcast_to([128, D]), writes=["lnc"], key="c0")
    wov = wout_d.rearrange("(kc p) n -> p kc n", p=128)
    for kc in range(8):
        st.dma("pool", wo[:, kc, :], wov[:, kc, :], writes=[("wo", kc)], key="wo")
    yTv = yT_d.rearrange("(kc p) t -> p kc t", p=128)
    x1Tv = x1T_d.rearrange("(kc p) t -> p kc t", p=128)
    for ti in range(NT):
        ta = ti * 128
        s2 = ti % 2
        s3 = ti % 3
        g4 = (ti // 4) % 2
        sub = ti % 4
        if sub == 0:
            st.dma("sync", yt[g4][:], yTv[:, :, ta:ta + 512], writes=[("yt", g4)], key=("yt", g4))
        st.dma("sync", xr[s2][:], xres_d[ta:ta + 128, :], writes=[("xr", s2)], key=("xr", s2))
        for c in range(2):
            pi = 2 * s2 + c
            for kc in range(8):
                st.op("pe", lambda e, c=c, kc=kc, g4=g4, sub=sub, pi=pi: e.matmul(ph[pi][:], lhsT=yt[g4][:, kc, sub * 128:(sub + 1) * 128], rhs=wo[:, kc, c * 512:(c + 1) * 512], start=(kc == 0), stop=(kc == 7)),
                      reads=[("yt", g4), ("wo", kc)], writes=[("ph", pi)])
            st.op("dve", lambda e, c=c, s2=s2, s3=s3, pi=pi: e.scalar_tensor_tensor(out=u[s3][:, c * 512:(c + 1) * 512], in0=xr[s2][:, c * 512:(c + 1) * 512], scalar=ALPHA, in1=ph[pi][:], op0=ALU.mult, op1=ALU.add),
                  reads=[("xr", s2), ("ph", pi)], writes=[("u", s3)])
        layer_norm_tile(st, u[s3], ("u", s3), u[s3], ("u", s3), g1, b1, "ln1", s3)
        st.dma("sync", x1_d[ta:ta + 128, :], u[s3][:], reads=[("u", s3)], key=("x1o", s3))
        for half in range(2):
            p = pt[half]
            for q in range(4):
                kc = half * 4 + q
                st.op("pe", lambda e, p=p, q=q, kc=kc, s3=s3: e.transpose(p[:, q * 128:(q + 1) * 128], u[s3][:, kc * 128:(kc + 1) * 128], idf[:]),
                      reads=[("u", s3), "idf"], writes=[("pt", half)])
            st.op("act", lambda e, p=p, half=half, g4=g4, sub=sub: e.copy(out=xT[g4][:, half * 4:half * 4 + 4, sub * 128:(sub + 1) * 128], in_=p[:].rearrange("p (a b) -> p a b", a=4)),
                  reads=[("pt", half)], writes=[("xT", g4)])
        if sub == 3:
            st.dma("sync", x1Tv[:, :, ta - 384:ta + 128], xT[g4][:], reads=[("xT", g4)], key=("xTo", g4))
    st.finish()


def stage_mlp(prog, name, x1_d, x1T_d, w1_d, w2_d, g2_d, b2_d, out_d):
    st = Stage(prog, name)
    TB = 256
    w1 = st.sb("w1", [128, 8, 4 * D], BF16)
    w2 = st.sb("w2", [128, 32, D], BF16)
    g2 = st.sb("g2", [128, D], F32)
    b2 = st.sb("b2", [128, D], F32)
    x1T = [st.sb(f"x1T{i}", [128, 8, 512], BF16) for i in range(2)]
    x1 = [st.sb(f"x1_{i}", [128, D], F32) for i in range(2)]
    aT = st.sb("aT", [128, 32, TB], BF16)
    rl = [st.sb(f"rl{i}", [128, TB], F32) for i in range(2)]
    v_ = [st.sb(f"v{i}", [128, D], F32) for i in range(2)]
    st.lnbuf = [(st.sb(f"lns{i}", [128, 12], F32), st.sb(f"lnm{i}", [128, 2], F32), st.sb(f"lnr{i}", [128, 1], F32)) for i in range(2)]
    pm = [st.ps(f"pm{i}", [128, 512], F32) for i in range(2)]
    po = [st.ps(f"po{i}", [128, 512], F32) for i in range(4)]
    for (dst, src) in ((g2, g2_d), (b2, b2_d)):
        st.dma("sync", dst[:], src.broadcast_to([128, D]), writes=["lnc"], key="c0")
    w1v = w1_d.rearrange("(kc p) n -> p kc n", p=128)
    for cb_ in range(4):
        for kc in range(8):
            st.dma("pool", w1[:, kc, cb_ * 1024:(cb_ + 1) * 1024], w1v[:, kc, cb_ * 1024:(cb_ + 1) * 1024], writes=[("w1", cb_)], key=("w1", cb_))
    w2v = w2_d.rearrange("(hc p) n -> p hc n", p=128)
    for hc in range(32):
        st.dma("pool", w2[:, hc, :], w2v[:, hc, :], writes=[("w2", hc // 8)], key=("w2", hc // 8))
    x1Tv = x1T_d.rearrange("(kc p) t -> p kc t", p=128)
    oi = 0
    for tt in range(T // TB):
        t0 = tt * TB
        sx = (tt // 2) % 2
        hx = (tt % 2) * TB
        if tt % 2 == 0:
            st.dma("sync", x1T[sx][:], x1Tv[:, :, t0:t0 + 512], writes=[("x1T", sx)], key=("x1T", sx))
        for hc in range(32):
            p = pm[hc % 2]
            pk = ("pm", hc % 2)
            for kc in range(8):
                st.op("pe", lambda e, p=p, kc=kc, hc=hc, sx=sx, hx=hx: e.matmul(p[:, 0:TB], lhsT=w1[:, kc, hc * 128:(hc + 1) * 128], rhs=x1T[sx][:, kc, hx:hx + TB], start=(kc == 0), stop=(kc == 7)),
                      reads=[("x1T", sx), ("w1", hc // 8)], writes=[pk])
            r_ = rl[hc % 2]
            rk = ("rl", hc % 2)
            st.op("act", lambda e, p=p, r_=r_: e.activation(out=r_[:], in_=p[:, 0:TB], func=AF.Relu), reads=[pk], writes=[rk])
            st.op("pool", lambda e, r_=r_, hc=hc: e.tensor_tensor(out=aT[:, hc, :], in0=r_[:], in1=r_[:], op=ALU.mult), reads=[rk], writes=[("aT", hc)])
        for sub in range(TB // 128):
            ta = t0 + sub * 128
            s2 = oi % 2
            st.dma("sync", x1[s2][:], x1_d[ta:ta + 128, :], writes=[("x1", s2)], key=("x1", s2))
            for c in range(2):
                pi = 2 * s2 + c
                for hc in range(32):
                    st.op("pe", lambda e, c=c, hc=hc, sub=sub, pi=pi: e.matmul(po[pi][:], lhsT=aT[:, hc, sub * 128:(sub + 1) * 128], rhs=w2[:, hc, c * 512:(c + 1) * 512], start=(hc == 0), stop=(hc == 31)),
                          reads=[("aT", hc), ("w2", hc // 8)], writes=[("po", pi)])
                st.op("dve", lambda e, c=c, s2=s2, pi=pi: e.scalar_tensor_tensor(out=v_[s2][:, c * 512:(c + 1) * 512], in0=x1[s2][:, c * 512:(c + 1) * 512], scalar=ALPHA, in1=po[pi][:], op0=ALU.mult, op1=ALU.add),
                      reads=[("x1", s2), ("po", pi)], writes=[("v", s2)])
            layer_norm_tile(st, v_[s2], ("v", s2), v_[s2], ("v", s2), g2, b2, "ln2", s2)
            st.dma("sync", out_d[ta:ta + 128, :], v_[s2][:], reads=[("v", s2)], key=("vo", s2))
            oi += 1
    st.finish()


SCALE = 128 ** -0.5
DIL = (1, 4, 16)


def stage_dil(prog, name, dq_d, dk_d, dv_d, oT_d, consts, nS=4, nN=4, NP=4):
    st = Stage(prog, name)
    st.costs = {"pe": 0.09, "act": 0.36, "dve": 0.66, "pool": 2.8}
    st.prio_cp = True
    identb = st.sb("identb", [128, 128], BF16)
    onesb = st.sb("onesb", [128, 128], BF16)
    mask = st.sb("mask", [128, 256], BF16)
    kT = [st.sb(f"kT{i}", [128, T], BF16) for i in range(2)]
    qT = [[st.sb(f"qT{i}_{g}", [128, T], BF16) for g in range(3)] for i in range(2)]
    Vd = [[st.sb(f"V{i}_{g}", [128, T // 128 // DIL[g], DIL[g], 128], BF16) for g in range(3)] for i in range(2)]
    accs = [st.sb(f"acc{i}", [128, 2, T], F32) for i in range(2)]
    rd = st.sb("rd", [128, 1024], F32)
    ot = [st.sb(f"ot{i}", [128, 1024], BF16) for i in range(2)]
    pT = [st.sb(f"pT{i}", [128, 512], BF16) for i in range(NP)]
    pS = [st.ps(f"pS{i}", [128, 512], F32) for i in range(nS)]
    pN = [st.ps(f"pN{i}", [128, 512], F32) for i in range(nN)]
    st.dma("pool", identb[:], consts["ident"][:, :], writes=["c"], key="c0")
    st.dma("pool", onesb[:], consts["ones"][:, :], writes=["c"], key="c0")
    st.dma("pool", mask[:], consts["dmask"][:, :], writes=["c"], key="c0")
    bi = 0
    oi = 0
    for h in range(8):
        sl = h % 2
        lk = ("ld", sl)
        acc = accs[h % 2]
        st.dma("sync", kT[sl][:], dk_d[h], writes=[lk], key=lk)
        for g in range(3):
            st.dma("sync", qT[sl][g][:], dq_d[g * 8 + h], writes=[lk], key=lk)
        for g, d in enumerate(DIL):
            src = dv_d[h].rearrange("(n p r) c -> p n (r c)", p=128, r=d)
            st.dma("sync", Vd[sl][g][:].rearrange("p n r c -> p n (r c)"), src, writes=[lk], key=lk)
        for g, d in enumerate(DIL):
            nblk = T // d // 128
            for r in range(d):
                for n0 in range(0, nblk, 2):
                    def cols(nn, r=r, d=d, cnt=128):
                        s0 = r + d * 128 * nn
                        return slice(s0, s0 + d * (cnt - 1) + 1, d)
                    si = bi % nS
                    S = pS[si]
                    Sk = ("pS", si)
                    for u in range(2):
                        n = n0 + u
                        qv = qT[sl][g][:, cols(n)]
                        c0 = u * 256
                        if n > 0:
                            st.op("pe", lambda e, S=S, n=n, qv=qv, cols=cols, sl=sl, c0=c0: e.matmul(S[:, c0:c0 + 128], lhsT=kT[sl][:, cols(n - 1)], rhs=qv, start=True, stop=False),
                                  reads=[lk], writes=[Sk])
                            st.op("pe", lambda e, S=S, c0=c0: e.matmul(S[:, c0:c0 + 128], lhsT=identb[:], rhs=mask[:, 0:128], start=False, stop=True),
                                  reads=["c"], writes=[Sk])
                        st.op("pe", lambda e, S=S, n=n, qv=qv, cols=cols, sl=sl, c0=c0: e.matmul(S[:, c0 + 128:c0 + 256], lhsT=kT[sl][:, cols(n)], rhs=qv, start=True, stop=False),
                              reads=[lk], writes=[Sk])
                        st.op("pe", lambda e, S=S, c0=c0: e.matmul(S[:, c0 + 128:c0 + 256], lhsT=identb[:], rhs=mask[:, 128:256], start=False, stop=True),
                              reads=["c"], writes=[Sk])
                    lo = 0 if n0 > 0 else 128
                    pi = bi % NP
                    P = pT[pi]
                    Pk = ("pT", pi)
                    st.op("act", lambda e, S=S, P=P, lo=lo: e.activation(out=P[:, lo:512], in_=S[:, lo:512], func=AF.Exp, scale=SCALE),
                          reads=[Sk], writes=[Pk, Sk], c=0.6)
                    ni = bi % nN
                    N_ = pN[ni]
                    Nk = ("pN", ni)
                    for half in range(2):
                        for u in range(2):
                            n = n0 + u
                            vb = r * nblk + n
                            o0 = half * 256 + u * 128
                            c0 = u * 256
                            if half == 0:
                                lp = Vd[sl][g][:, n - 1, r, :] if n > 0 else None
                                lc = Vd[sl][g][:, n, r, :]
                            else:
                                lp = onesb[:] if n > 0 else None
                                lc = onesb[:]
                            if n > 0:
                                st.op("pe", lambda e, N_=N_, o0=o0, lp=lp, P=P, c0=c0: e.matmul(N_[:, o0:o0 + 128], lhsT=lp, rhs=P[:, c0:c0 + 128], start=True, stop=False),
                                      reads=[Pk, lk, "c"], writes=[Nk])
                            st.op("pe", lambda e, N_=N_, o0=o0, lc=lc, P=P, n=n, c0=c0: e.matmul(N_[:, o0:o0 + 128], lhsT=lc, rhs=P[:, c0 + 128:c0 + 256], start=(n == 0), stop=True),
                                  reads=[Pk, lk, "c"], writes=[Nk])
                    av = acc[:, :, cols(n0, cnt=256)]
                    nv = N_[:, :].rearrange("p (a b) -> p a b", a=2)
                    n = n0
                    ablk = (n // 16) if g == 0 else ((n // 4) if g == 1 else 0)
                    acks = [("acc", h % 2, ablk)] if g < 2 else [("acc", h % 2, 0), ("acc", h % 2, 1)]
                    if g == 0:
                        st.op("act", lambda e, av=av, nv=nv: e.copy(out=av, in_=nv), reads=[Nk], writes=acks + [Nk], c=0.6)
                    else:
                        st.op("dve", lambda e, av=av, nv=nv: e.tensor_tensor(out=av, in0=nv, in1=av, op=ALU.add), reads=[Nk] + acks, writes=acks + [Nk], c=1.2)
                    bi += 1
        for c in range(4):
            cs = slice(c * 1024, (c + 1) * 1024)
            o_ = ot[oi % 2]
            ok = ("ot", oi % 2)
            ack = ("acc", h % 2, c // 2)
            st.op("act", lambda e, cs=cs, acc=acc: e.activation(out=rd[:], in_=acc[:, 1, cs], func=AF.Ln), reads=[ack], writes=["rd"], c=0.9)
            st.op("act", lambda e: e.activation(out=rd[:], in_=rd[:], func=AF.Exp, scale=-1.0), reads=["rd"], writes=["rd"], c=0.9)
            st.op("pool", lambda e, cs=cs, o_=o_, acc=acc: e.tensor_tensor(out=o_[:], in0=acc[:, 0, cs], in1=rd[:], op=ALU.mult), reads=[ack, "rd"], writes=[ok], c=3.0)
            st.dma("sync", oT_d[h * 128:(h + 1) * 128, cs], o_[:], reads=[ok], key=ok)
            oi += 1
    st.finish()


def stage_moba(prog, name, mq_d, mk_d, mv_d, yT_d, consts, st=None, nS=3, nN=2, nG=1, nld=2, finish=True, NP=4):
    if st is None:
        st = Stage(prog, name)
    identb = st.sb("identb", [128, 128], BF16)
    identf = st.sb("identf", [128, 128], F32)
    onesb = st.sb("onesb", [128, 128], BF16)
    cmask = st.sb("cmask", [128, 2, 256], BF16)
    selall = st.sb("selall", [16, 16 * 128], BF16)
    kT = [st.sb(f"kT{i}", [128, T], BF16) for i in range(nld)]
    qT = [st.sb(f"qT{i}", [128, T], BF16) for i in range(nld)]
    V = [st.sb(f"V{i}", [128, 32, 128], BF16) for i in range(nld)]
    nmT = st.sb("nmT", [16, T], BF16)
    ks = st.sb("ks", [128, 16], F32)
    khi = st.sb("khi", [128, 16], BF16)
    klo = st.sb("klo", [128, 16], BF16)
    gm = [st.sb(f"gm{i}", [128, 16], F32) for i in range(2)]
    m8 = [st.sb(f"m8{i}", [128, 8], F32) for i in range(2)]
    nm = [st.sb(f"nm{i}", [128, 16], F32) for i in range(2)]
    rd = st.sb("rd", [128, 256], F32)
    dacc = [st.sb(f"dacc{i}", [128, 256], F32) for i in range(2)]
    daccb = [st.sb(f"daccb{i}", [128, 256], BF16) for i in range(2)]
    daccl = [st.sb(f"daccl{i}", [128, 256], BF16) for i in range(2)]
    ot = [st.sb(f"ot{i}", [128, 256], BF16) for i in range(2)]
    pT = [st.sb(f"pT{i}", [128, 256], BF16) for i in range(NP)]
    pS = [st.ps(f"pS{i}", [128, 512], F32) for i in range(nS)]
    pNum = [st.ps(f"pNu{i}", [128, 512], F32) for i in range(nN)]
    pDen = [st.ps(f"pDe{i}", [128, 512], F32) for i in range(nN)]
    pG = [st.ps(f"pG{i}", [128, 512], F32) for i in range(nG)]
    st.dma("pool", identb[:], consts["ident"][:, :], writes=["c"], key="c0")
    st.dma("sync", identf[:], consts["ident"][:, :], writes=["c"], key="c1")
    st.dma("pool", onesb[:], consts["ones"][:, :], writes=["c"], key="c0")
    st.dma("pool", cmask[:], consts["cmask"][:, :, :], writes=["c"], key="c0")
    st.dma("pool", selall[:], consts["selall"][:, :], writes=["c"], key="c0")
    bi = 0
    gi = 0
    for h in range(4):
        sl = h % nld
        lk = ("ld", sl)
        st.dma("sync", kT[sl][:], mk_d[h], writes=[lk], key=lk)
        st.dma("sync", qT[sl][:], mq_d[h], writes=[lk], key=lk)
        st.dma("sync", V[sl][:], mv_d[:, h * 128:(h + 1) * 128].rearrange("(n p) c -> p n c", p=128), writes=[lk], key=lk)
        st.op("dve", lambda e, sl=sl: e.tensor_reduce(out=ks[:], in_=kT[sl][:].rearrange("p (n k) -> p n k", k=256), axis=AX.X, op=ALU.add), reads=[lk], writes=["ks"])
        st.op("dve", lambda e: e.tensor_copy(out=khi[:], in_=ks[:]), reads=["ks"], writes=["khi"])
        st.op("dve", lambda e: e.tensor_tensor(out=klo[:], in0=ks[:], in1=khi[:], op=ALU.subtract), reads=["ks", "khi"], writes=["klo"])
        for qt in range(2, 32):
            i = qt // 2
            G = pG[gi % nG]
            Gk = ("pG", gi % nG)
            g_ = gm[gi % 2]
            m_ = m8[gi % 2]
            n_ = nm[gi % 2]
            sk = ("gs", gi % 2)
            st.op("pe", lambda e, G=G, qt=qt, sl=sl: e.matmul(G[:, 0:16], lhsT=qT[sl][:, qt * 128:(qt + 1) * 128], rhs=khi[:], start=True, stop=False), reads=[lk, "khi"], writes=[Gk])
            st.op("pe", lambda e, G=G, qt=qt, sl=sl: e.matmul(G[:, 0:16], lhsT=qT[sl][:, qt * 128:(qt + 1) * 128], rhs=klo[:], start=False, stop=True), reads=[lk, "klo"], writes=[Gk])
            st.op("dve", lambda e, G=G, g_=g_, i=i: e.tensor_copy(out=g_[:, 0:i], in_=G[:, 0:i]), reads=[Gk], writes=[sk, Gk])
            if i < 16:
                st.op("dve", lambda e, g_=g_, i=i: e.memset(g_[:, i:16], -1e30), reads=[], writes=[sk])
            st.op("dve", lambda e, g_=g_, m_=m_: e.max(out=m_[:], in_=g_[:]), reads=[sk], writes=[sk])
            st.op("dve", lambda e, g_=g_, m_=m_, n_=n_: e.tensor_scalar(out=n_[:], in0=g_[:], scalar1=m_[:, 2:3], scalar2=NEG, op0=ALU.is_lt, op1=ALU.mult), reads=[sk], writes=[sk])
            st.op("pe", lambda e, G=G, n_=n_: e.transpose(G[0:16, 128:256], n_[:, :], identf[:]), reads=[sk, "c"], writes=[Gk])
            st.op("act", lambda e, G=G, qt=qt: e.copy(out=nmT[:, qt * 128:(qt + 1) * 128], in_=G[0:16, 128:256]), reads=[Gk], writes=[("nmT", qt // 2), Gk])
            gi += 1
        for i in range(16):
            qs = slice(i * 256, (i + 1) * 256)
            Nu = pNum[i % nN]
            De = pDen[i % nN]
            Nuk = ("pNu", i % nN)
            Dek = ("pDe", i % nN)
            nkc = 2 * (i + 1)
            for j in range(i + 1):
                for c in range(2):
                    kc = 2 * j + c
                    si = bi % nS
                    S = pS[si]
                    Sk = ("pS", si)
                    st.op("pe", lambda e, S=S, kc=kc, qs=qs, sl=sl: e.matmul(S[:, 0:256], lhsT=kT[sl][:, kc * 128:(kc + 1) * 128], rhs=qT[sl][:, qs], start=True, stop=False),
                          reads=[lk], writes=[Sk])
                    if j < i:
                        st.op("pe", lambda e, S=S, j=j, qs=qs: e.matmul(S[:, 0:256], lhsT=selall[:, j * 128:(j + 1) * 128], rhs=nmT[:, qs], start=False, stop=True),
                              reads=["c", ("nmT", i)], writes=[Sk])
                    else:
                        st.op("pe", lambda e, S=S, c=c: e.matmul(S[:, 0:256], lhsT=identb[:], rhs=cmask[:, c, :], start=False, stop=True),
                              reads=["c"], writes=[Sk])
                    pi = bi % NP
                    P = pT[pi]
                    Pk = ("pT", pi)
                    st.op("act", lambda e, S=S, P=P: e.activation(out=P[:], in_=S[:, 0:256], func=AF.Exp, scale=SCALE), reads=[Sk], writes=[Pk, Sk])
                    st.op("pe", lambda e, Nu=Nu, kc=kc, P=P, sl=sl, nkc=nkc: e.matmul(Nu[:, 0:256], lhsT=V[sl][:, kc, :], rhs=P[:], start=(kc == 0), stop=(kc == nkc - 1)),
                          reads=[Pk, lk], writes=[Nuk])
                    st.op("pe", lambda e, De=De, kc=kc, P=P, nkc=nkc: e.matmul(De[:, 0:256], lhsT=onesb[:], rhs=P[:], start=(kc == 0), stop=(kc == nkc - 1)),
                          reads=[Pk, "c"], writes=[Dek])
                    bi += 1
            o_ = ot[i % 2]
            ok = ("ot", i % 2)
            st.op("dve", lambda e, De=De: e.reciprocal(out=rd[:], in_=De[:, 0:256]), reads=[Dek], writes=["rd", Dek])
            st.op("dve", lambda e, Nu=Nu, o_=o_: e.tensor_tensor(out=o_[:], in0=Nu[:, 0:256], in1=rd[:], op=ALU.mult), reads=[Nuk, "rd"], writes=[ok, Nuk])
            st.dma("sync", yT_d[512 + h * 128:512 + (h + 1) * 128, qs], o_[:], reads=[ok], key=ok)
    if finish:
        st.finish()


NCH = T // 128
DEC = -0.6065306597126334
GN_EPS = 64e-5
import os
FILLER = False
STOP = int(os.environ.get('RW_STOP', '0'))


def stage_rwkv(prog, name, zr_d, prm, yT_d, consts, nch=NCH, st=None, nbank=8, finish=True):
    if st is None:
        st = Stage(prog, name)
    nc = st.nc

    st.costs = {"act": 0.55, "dve": 0.8, "pool": 1.5, "pe": 0.15}
    st.prio_cp = True

    def cb(nm, src, n):
        t = st.sb(nm, [128, n], F32)
        st.dma("sync", t[:], src.broadcast_to([128, n]), writes=["c"], key="c0")
        return t
    mixb = cb("mixb", prm["shift_mix"], 1824)
    w0b = cb("w0b", prm["w0"], 512)
    a0b = cb("a0b", prm["a0"], 512)
    kkb = cb("kkb", prm["k_k"], 512)
    kab = cb("kab", prm["k_a"], 512)
    rkb = cb("rkb", prm["r_k"], 512)
    lgb = cb("lgb", prm["lnx_g"], 512)
    lbb = cb("lbb", prm["lnx_b"], 512)
    Wwa = st.sb("Wwa", [128, 512], F32)
    Wg = st.sb("Wg", [128, 2, 512], F32)
    st.dma("sync", Wwa[0:64, :], prm["w_up"][:, :], writes=["c"], key="c0")
    st.dma("sync", Wwa[64:128, :], prm["a_up"][:, :], writes=["c"], key="c0")
    st.dma("sync", Wg[:, 0, :], prm["g_up"][0:128, :], writes=["c"], key="c0")
    st.dma("sync", Wg[0:32, 1, :], prm["g_up"][128:160, :], writes=["c"], key="c0")
    identf = st.sb("identf", [128, 128], F32)
    identb = st.sb("identb", [128, 128], BF16)
    tri = st.sb("tri", [128, 128], F32)
    onesf = st.sb("onesf", [128, 128], F32)
    mk12 = st.sb("mk12", [128, 512], BF16)
    mk3 = st.sb("mk3", [128, 512], BF16)
    st.dma("sync", identf[:], consts["ident"][:, :], writes=["c"], key="c0")
    st.dma("pool", identb[:], consts["ident"][:, :], writes=["c"], key="c1")
    st.dma("sync", tri[:], consts["tri"][:, :], writes=["c"], key="c0")
    st.dma("sync", onesf[:], consts["ones"][:, :], writes=["c"], key="c0")
    st.dma("pool", mk12[:], consts["mk12"][:, :], writes=["c"], key="c1")
    st.dma("pool", mk3[:], consts["mk3"][:, :], writes=["c"], key="c1")

    H = [st.sb(f"H{i}", [128, 64], F32) for i in range(4)]
    Hb = [st.sb(f"Hb{i}", [128, 64], BF16) for i in range(4)]
    for i in range(4):
        st.op("pool", lambda e, i=i: e.memset(H[i][:], 0.0), writes=[("H", i)])
        st.op("pool", lambda e, i=i: e.memset(Hb[i][:], 0.0), writes=[("Hb", i)])

    PB = [st.ps(f"pb{i}", [128, 512], F32) for i in range(nbank)]
    POOLS = ({"p": [0, 1], "i0": [2, 3, 4], "i1": [2, 3, 4], "s": [5, 6]} if FILLER else {"p": [0, 1, 2], "i0": [3, 4, 5], "i1": [3, 4, 5], "s": [6, 7]}) if nbank >= 8 else {"p": list(range(nbank)), "i": list(range(nbank)), "s": list(range(nbank))}
    bank_ctr = {"p": 0, "i0": 0, "i1": 0, "s": 0}
    cur_pool = ["p"]

    def bank():
        pl = cur_pool[0]
        if nbank < 8:
            k_ = bank_ctr["p"]
            bank_ctr["p"] += 1
            i = k_ % nbank
        else:
            lst = POOLS[pl]
            i = lst[bank_ctr[pl] % len(lst)]
            bank_ctr[pl] += 1
        return PB[i], ("pb", i)

    tiles = {}

    NBUF = {}

    def tl(nm, shape, dt, nb=2):
        t_ = [st.sb(f"{nm}{p}", shape, dt) for p in range(nb)]
        tiles[nm] = [t_[p % nb] for p in range(2)]
        NBUF[nm] = nb
    single = ("t0", "t1", "t2", "Ysq", "kka", "LT", "sm", "gam", "bon", "Ysb")
    for nm in ("z", "zp", "zs"):
        tl(nm, [128, 1824], F32, 2)
    tl("LT", [128, 384], F32, 1)
    for nm in ("sgw", "a", "g", "Ep", "En", "Ex", "Et", "t0", "t1", "t2", "kk", "k2", "kka", "Ysb", "Ysq", "ya", "bon"):
        tl(nm, [128, 512], F32, 1 if nm in single else 2)
    for nm in ("RT", "AT", "BT_", "KT_", "Bh", "Kh", "vb", "Zs", "Us"):
        tl(nm, [128, 512], BF16, 2)
    tl("ART", [128, 4, 2, 128], BF16)
    tl("BTT", [128, 4, 128], BF16)
    tl("KTT", [128, 4, 128], BF16)
    tl("ABT", [128, 8, 2, 128], BF16)
    tl("AKT", [128, 8, 2, 128], BF16)
    for nm in ("A0", "A1", "M0", "M1", "P0", "P1"):
        tl(nm, [128, 8, 128], BF16, 2)
    tl("sm", [128, 64], F32, 2)
    tl("gam", [128, 4], F32, 2)
    yaT4 = [st.sb(f"yaT4_{i}", [128, 4, 512], BF16) for i in range(2)]

    def do_chunk(c):
        p = c % 2
        t0 = c * 128
        X = {k: v[p] for k, v in tiles.items()}
        K = lambda nm: (nm, p if NBUF[nm] == 2 else 0)
        z, zp, zs = X["z"], X["zp"], X["zs"]
        st.dma("sync", z[:], zr_d[t0:t0 + 128, :], writes=[K("z")], key=("z", p))
        if c == 0:
            st.op("pool", lambda e, zp=zp: e.memset(zp[:], 0.0), writes=[K("zp")])
            st.dma("sync", zp[1:128, :], zr_d[0:127, :], reads=[K("zp")], writes=[K("zp")], key=("zp", p))
        else:
            st.dma("sync", zp[:], zr_d[t0 - 1:t0 + 127, :], writes=[K("zp")], key=("zp", p))
        st.op("dve", lambda e, z=z, zp=zp, zs=zs: e.tensor_tensor(out=zs[:], in0=zp[:], in1=z[:], op=ALU.subtract), reads=[K("z"), K("zp")], writes=[K("zs")])
        st.op("pool", lambda e, zs=zs: e.tensor_tensor(out=zs[:], in0=zs[:], in1=mixb[:], op=ALU.mult), reads=[K("zs"), "c"], writes=[K("zs")])
        st.op("dve", lambda e, z=z, zs=zs: e.tensor_tensor(out=zs[:], in0=zs[:], in1=z[:], op=ALU.add), reads=[K("zs"), K("z")], writes=[K("zs")])
        r = zs[:, 0:512]
        k = zs[:, 512:1024]
        v = zs[:, 1024:1536]
        st.op("act", lambda e, zs=zs: e.activation(out=zs[:, 1536:1600], in_=zs[:, 1536:1600], func=AF.Tanh), reads=[K("zs")], writes=[K("zs")])
        st.op("act", lambda e, zs=zs: e.activation(out=zs[:, 1664:1824], in_=zs[:, 1664:1824], func=AF.Sigmoid), reads=[K("zs")], writes=[K("zs")])
        cur_pool[0] = "p"
        bL, bLk = bank()
        st.op("pe", lambda e, bL=bL, zs=zs: e.transpose(bL[:, 0:128], zs[:, 1536:1664], identf[:]), reads=[K("zs"), "c"], writes=[bLk])
        st.op("pe", lambda e, bL=bL, zs=zs: e.transpose(bL[:, 128:256], zs[:, 1664:1792], identf[:]), reads=[K("zs"), "c"], writes=[bLk])
        st.op("pe", lambda e, bL=bL, zs=zs: e.transpose(bL[0:32, 256:384], zs[:, 1792:1824], identf[:]), reads=[K("zs"), "c"], writes=[bLk])
        LT = X["LT"]
        st.op("act", lambda e, bL=bL, LT=LT: e.copy(out=LT[:, 0:256], in_=bL[:, 0:256]), reads=[bLk], writes=[K("LT"), bLk])
        st.op("act", lambda e, bL=bL, LT=LT: e.copy(out=LT[0:32, 256:384], in_=bL[0:32, 256:384]), reads=[bLk], writes=[K("LT"), bLk])
        if STOP == 1:
            return
        bW, bWk = bank()
        bA, bAk = bank()
        st.op("pe", lambda e, bW=bW, LT=LT: e.matmul(bW[:], lhsT=LT[0:64, 0:128], rhs=Wwa[0:64, :], start=True, stop=True), reads=[K("LT"), "c"], writes=[bWk])
        st.op("pe", lambda e, bA=bA, LT=LT: e.matmul(bA[:], lhsT=LT[64:128, 0:128], rhs=Wwa[64:128, :], start=True, stop=True), reads=[K("LT"), "c"], writes=[bAk])
        sgw, a_, g_ = X["sgw"], X["a"], X["g"]
        st.op("dve", lambda e, bW=bW, sgw=sgw: e.tensor_tensor(out=sgw[:], in0=bW[:], in1=w0b[:], op=ALU.add), reads=[bWk, "c"], writes=[K("sgw"), bWk])
        st.op("act", lambda e, sgw=sgw: e.activation(out=sgw[:], in_=sgw[:], func=AF.Sigmoid), reads=[K("sgw")], writes=[K("sgw")])
        st.op("dve", lambda e, bA=bA, a_=a_: e.tensor_tensor(out=a_[:], in0=bA[:], in1=a0b[:], op=ALU.add), reads=[bAk, "c"], writes=[K("a"), bAk])
        st.op("act", lambda e, a_=a_: e.activation(out=a_[:], in_=a_[:], func=AF.Sigmoid), reads=[K("a")], writes=[K("a")])
        bG, bGk = bank()
        st.op("pe", lambda e, bG=bG, LT=LT: e.matmul(bG[:], lhsT=LT[:, 128:256], rhs=Wg[:, 0, :], start=True, stop=False), reads=[K("LT"), "c"], writes=[bGk])
        st.op("pe", lambda e, bG=bG, LT=LT: e.matmul(bG[:], lhsT=LT[0:32, 256:384], rhs=Wg[0:32, 1, :], start=False, stop=True), reads=[K("LT"), "c"], writes=[bGk])
        st.op("act", lambda e, bG=bG, g_=g_: e.copy(out=g_[:], in_=bG[:]), reads=[bGk], writes=[K("g"), bGk])
        if STOP == 2:
            return
        Ep, En, Ex, Et, gam = X["Ep"], X["En"], X["Ex"], X["Et"], X["gam"]
        bGm, bGmk = bank()
        for hp in range(4):
            st.op("pe", lambda e, bGm=bGm, sgw=sgw, hp=hp: e.matmul(bGm[:, hp:hp + 1], lhsT=sgw[:, hp * 128:(hp + 1) * 128], rhs=onesf[:, 0:1], start=True, stop=True), reads=[K("sgw"), "c"], writes=[bGmk])
        st.op("act", lambda e, bGm=bGm, gam=gam: e.activation(out=gam[:], in_=bGm[:, 0:4], func=AF.Exp, scale=DEC), reads=[bGmk], writes=[K("gam"), bGmk])
        bC, bCk = bank()
        bT_, bTk = bank()
        st.op("pe", lambda e, bC=bC, sgw=sgw: e.matmul(bC[:], lhsT=tri[:], rhs=sgw[:], start=True, stop=True), reads=[K("sgw"), "c"], writes=[bCk])
        st.op("pe", lambda e, bT_=bT_, sgw=sgw: e.matmul(bT_[:], lhsT=onesf[:], rhs=sgw[:], start=True, stop=True), reads=[K("sgw"), "c"], writes=[bTk])
        st.op("act", lambda e, bC=bC, Ep=Ep: e.activation(out=Ep[:], in_=bC[:], func=AF.Exp, scale=DEC), reads=[bCk], writes=[K("Ep"), bCk])
        st.op("act", lambda e, bC=bC, En=En: e.activation(out=En[:], in_=bC[:], func=AF.Exp, scale=-DEC), reads=[bCk], writes=[K("En"), bCk])
        st.op("act", lambda e, bT_=bT_, Et=Et: e.activation(out=Et[:], in_=bT_[:], func=AF.Exp, scale=DEC), reads=[bTk], writes=[K("Et"), bTk])
        st.op("act", lambda e, sgw=sgw, Ex=Ex: e.activation(out=Ex[:], in_=sgw[:], func=AF.Exp, scale=-DEC), reads=[K("sgw")], writes=[K("Ex")])
        st.op("pool", lambda e, Ex=Ex, Ep=Ep: e.tensor_tensor(out=Ex[:], in0=Ex[:], in1=Ep[:], op=ALU.mult), reads=[K("Ex"), K("Ep")], writes=[K("Ex")])
        st.op("pool", lambda e, Et=Et, En=En: e.tensor_tensor(out=Et[:], in0=Et[:], in1=En[:], op=ALU.mult), reads=[K("Et"), K("En")], writes=[K("Et")])
        if STOP == 3:
            return
        t0_, t1_, t2_, kk, k2, kka, sm = X["t0"], X["t1"], X["t2"], X["kk"], X["k2"], X["kka"], X["sm"]
        st.op("pool", lambda e, kk=kk, k=k: e.tensor_tensor(out=kk[:], in0=k, in1=kkb[:], op=ALU.mult), reads=[K("zs"), "c"], writes=[K("kk")])
        st.op("dve", lambda e, kk=kk, t0_=t0_: e.tensor_tensor(out=t0_[:], in0=kk[:], in1=kk[:], op=ALU.mult), reads=[K("kk")], writes=[K("t0")])
        st.op("dve", lambda e, t0_=t0_, sm=sm: e.tensor_reduce(out=sm[:, 0:8], in_=t0_[:].rearrange("p (h n) -> p h n", n=64), axis=AX.X, op=ALU.add), reads=[K("t0")], writes=[K("sm")])
        st.op("act", lambda e, sm=sm: e.sqrt(out=sm[:, 8:16], in_=sm[:, 0:8]), reads=[K("sm")], writes=[K("sm")])
        st.op("dve", lambda e, sm=sm: e.tensor_scalar_max(out=sm[:, 8:16], in0=sm[:, 8:16], scalar1=1e-12), reads=[K("sm")], writes=[K("sm")])
        st.op("dve", lambda e, sm=sm: e.reciprocal(out=sm[:, 8:16], in_=sm[:, 8:16]), reads=[K("sm")], writes=[K("sm")])
        st.op("dve", lambda e, kk=kk, sm=sm: e.tensor_tensor(out=kk[:].rearrange("p (h n) -> p h n", n=64), in0=kk[:].rearrange("p (h n) -> p h n", n=64),
                                                              in1=sm[:, 8:16].unsqueeze(2).to_broadcast([128, 8, 64]), op=ALU.mult), reads=[K("kk"), K("sm")], writes=[K("kk")])
        st.op("dve", lambda e, a_=a_, k2=k2: e.scalar_tensor_tensor(out=k2[:], in0=a_[:], scalar=-1.0, in1=kab[:], op0=ALU.add, op1=ALU.mult), reads=[K("a"), "c"], writes=[K("k2")])
        st.op("dve", lambda e, k2=k2, k=k: e.scalar_tensor_tensor(out=k2[:], in0=k2[:], scalar=1.0, in1=k, op0=ALU.add, op1=ALU.mult), reads=[K("k2"), K("zs")], writes=[K("k2")])
        st.op("pool", lambda e, kka=kka, kk=kk, a_=a_: e.tensor_tensor(out=kka[:], in0=kk[:], in1=a_[:], op=ALU.mult), reads=[K("kk"), K("a")], writes=[K("kka")])
        RT, AT, BT_, KT_, Bh, Kh, vb = X["RT"], X["AT"], X["BT_"], X["KT_"], X["Bh"], X["Kh"], X["vb"]
        st.op("pool", lambda e, RT=RT, r=r, Ep=Ep: e.tensor_tensor(out=RT[:], in0=r, in1=Ep[:], op=ALU.mult), reads=[K("zs"), K("Ep")], writes=[K("RT")])
        st.op("dve", lambda e, AT=AT, kk=kk, Ex=Ex: e.scalar_tensor_tensor(out=AT[:], in0=kk[:], scalar=-1.0, in1=Ex[:], op0=ALU.mult, op1=ALU.mult), reads=[K("kk"), K("Ex")], writes=[K("AT")])
        st.op("pool", lambda e, t1_=t1_, kka=kka, En=En: e.tensor_tensor(out=t1_[:], in0=kka[:], in1=En[:], op=ALU.mult), reads=[K("kka"), K("En")], writes=[K("t1")])
        st.op("dve", lambda e, t2_=t2_, k2=k2, En=En: e.tensor_tensor(out=t2_[:], in0=k2[:], in1=En[:], op=ALU.mult), reads=[K("k2"), K("En")], writes=[K("t2")])
        st.op("act", lambda e, BT_=BT_, t1_=t1_: e.copy(out=BT_[:], in_=t1_[:]), reads=[K("t1")], writes=[K("BT_")])
        st.op("act", lambda e, KT_=KT_, t2_=t2_: e.copy(out=KT_[:], in_=t2_[:]), reads=[K("t2")], writes=[K("KT_")])
        st.op("pool", lambda e, Bh=Bh, kka=kka, Et=Et: e.tensor_tensor(out=Bh[:], in0=kka[:], in1=Et[:], op=ALU.mult), reads=[K("kka"), K("Et")], writes=[K("Bh")])
        st.op("dve", lambda e, Kh=Kh, k2=k2, Et=Et: e.tensor_tensor(out=Kh[:], in0=k2[:], in1=Et[:], op=ALU.mult), reads=[K("k2"), K("Et")], writes=[K("Kh")])
        st.op("act", lambda e, vb=vb, v=v: e.copy(out=vb[:], in_=v), reads=[K("zs")], writes=[K("vb")])
        bon = X["bon"]
        st.op("pool", lambda e, bon=bon, r=r, k2=k2: e.tensor_tensor(out=bon[:], in0=r, in1=k2[:], op=ALU.mult), reads=[K("zs"), K("k2")], writes=[K("bon")])
        st.op("pool", lambda e, bon=bon: e.tensor_tensor(out=bon[:], in0=bon[:], in1=rkb[:], op=ALU.mult), reads=[K("bon"), "c"], writes=[K("bon")])
        st.op("dve", lambda e, bon=bon, sm=sm: e.tensor_reduce(out=sm[:, 16:24], in_=bon[:].rearrange("p (h n) -> p h n", n=64), axis=AX.X, op=ALU.add), reads=[K("bon")], writes=[K("sm")])
        st.op("dve", lambda e, bon=bon, sm=sm, v=v: e.tensor_tensor(out=bon[:].rearrange("p (h n) -> p h n", n=64), in0=v.rearrange("p (h n) -> p h n", n=64),
                                                                     in1=sm[:, 16:24].unsqueeze(2).to_broadcast([128, 8, 64]), op=ALU.mult), reads=[K("zs"), K("sm"), K("bon")], writes=[K("bon")])
        if STOP == 4:
            return
        ART, BTT, KTT = X["ART"], X["BTT"], X["KTT"]
        for ti, (src, sk, dst_fn, dk) in enumerate(((AT, K("AT"), lambda: ART[:, :, 0, :], K("ART")), (RT, K("RT"), lambda: ART[:, :, 1, :], K("ART")),
                                                    (BT_, K("BT_"), lambda: BTT[:, :, :], K("BTT")), (KT_, K("KT_"), lambda: KTT[:, :, :], K("KTT")))):
            bT2, hk = bank()
            PTv = bT2[:].bitcast(BF16)
            for hp in range(4):
                o_ = st.op("pe", lambda e, src=src, hp=hp, PTv=PTv: e.transpose(PTv[:, hp * 128:(hp + 1) * 128], src[:, hp * 128:(hp + 1) * 128], identb[:]),
                           reads=[sk, "c"], writes=[hk])
                if FILLER and st.pe_filler is None and nbank >= 8:
                    st.pe_filler = (lambda e: e.matmul(PB[7][:, 0:128], lhsT=identb[:], rhs=identb[:], start=True, stop=True), o_, 0.09, 300)
            eng = "act" if ti % 2 == 0 else "dve"
            if eng == "act":
                st.op("act", lambda e, dst_fn=dst_fn, PTv=PTv: e.copy(out=dst_fn(), in_=PTv[:, 0:512].rearrange("p (a b) -> p a b", a=4)), reads=[hk], writes=[dk, hk])
            else:
                st.op("dve", lambda e, dst_fn=dst_fn, PTv=PTv: e.tensor_copy(out=dst_fn(), in_=PTv[:, 0:512].rearrange("p (a b) -> p a b", a=4)), reads=[hk], writes=[dk, hk])
        if STOP == 5:
            return
        ABT, AKT = X["ABT"], X["AKT"]
        A_ = [X["A0"], X["A1"]]
        M_ = [X["M0"], X["M1"]]
        P_ = [X["P0"], X["P1"]]
        Ak = [K("A0"), K("A1")]
        Mk = [K("M0"), K("M1")]
        Pk = [K("P0"), K("P1")]
        for (lt_, lk_, dst, dkey) in ((BTT, K("BTT"), ABT, K("ABT")), (KTT, K("KTT"), AKT, K("AKT"))):
            for par in range(2):
                for q2 in range(2):
                    hA = 4 * q2 + par
                    b_, bk_ = bank()
                    b0 = 64 * par
                    for hh in range(2):
                        h = hA + 2 * hh
                        hp = h // 2
                        st.op("pe", lambda e, b_=b_, hh=hh, lt_=lt_, hp=hp, b0=b0: e.matmul(b_[:, hh * 256:(hh + 1) * 256], lhsT=lt_[b0:b0 + 64, hp, :], rhs=ART[b0:b0 + 64, hp, :, :].rearrange("p a b -> p (a b)"), start=True, stop=True),
                              reads=[lk_, K("ART")], writes=[bk_])
                    st.op("dve", lambda e, b_=b_, dst=dst, hA=hA: e.tensor_tensor(out=dst[:, hA:hA + 3:2, :, :], in0=b_[:].rearrange("p (a b c) -> p a b c", a=2, b=2), in1=mk12[:].rearrange("p (a b c) -> p a b c", a=2, b=2), op=ALU.mult),
                          reads=[bk_, "c"], writes=[dkey, bk_])
        for par in range(2):
            b_, bk_ = bank()
            b0 = 64 * par
            for hh in range(4):
                h = par + 2 * hh
                hp = h // 2
                st.op("pe", lambda e, b_=b_, hh=hh, hp=hp, b0=b0: e.matmul(b_[:, hh * 128:(hh + 1) * 128], lhsT=ART[b0:b0 + 64, hp, 0, :], rhs=BTT[b0:b0 + 64, hp, :], start=True, stop=True),
                      reads=[K("ART"), K("BTT")], writes=[bk_])
            st.op("dve", lambda e, b_=b_, par=par: e.tensor_tensor(out=A_[0][:, par:par + 7:2, :], in0=b_[:].rearrange("p (a b) -> p a b", a=4), in1=mk3[:].rearrange("p (a b) -> p a b", a=4), op=ALU.mult),
                  reads=[bk_, "c"], writes=[Ak[0], bk_])
        if STOP == 6:
            return
        cur_pool[0] = "i0"
        st.op("pool", lambda e: e.tensor_copy(out=M_[0][:], in_=ABT[:, :, 0, :]), reads=[K("ABT")], writes=[Mk[0]])
        st.op("pool", lambda e: e.tensor_tensor(out=P_[0][:], in0=ABT[:, :, 0, :], in1=identb[:].unsqueeze(1).to_broadcast([128, 8, 128]), op=ALU.add), reads=[K("ABT"), "c"], writes=[Pk[0]])
        cur = 0
        for lvl in range(1, 7):
            nxt = 1 - cur
            for hq in range(2):
                b_, bk_ = bank()
                for hh in range(4):
                    h = hq * 4 + hh
                    st.op("pe", lambda e, b_=b_, hh=hh, h=h, cur=cur: e.matmul(b_[:, hh * 128:(hh + 1) * 128], lhsT=M_[cur][:, h, :], rhs=A_[cur][:, h, :], start=True, stop=True),
                          reads=[Mk[cur], Ak[cur]], writes=[bk_])
                st.op("act", lambda e, b_=b_, hq=hq, nxt=nxt: e.copy(out=A_[nxt][:, hq * 4:hq * 4 + 4, :].rearrange("p a b -> p (a b)"), in_=b_[:]), reads=[bk_], writes=[Ak[nxt], bk_])
            if lvl < 6:
                for hq in range(2):
                    b_, bk_ = bank()
                    for hh in range(4):
                        h = hq * 4 + hh
                        st.op("pe", lambda e, b_=b_, hh=hh, h=h, cur=cur: e.matmul(b_[:, hh * 128:(hh + 1) * 128], lhsT=A_[cur][:, h, :], rhs=M_[cur][:, h, :], start=True, stop=True),
                              reads=[Mk[cur], Ak[cur]], writes=[bk_])
                    st.op("act", lambda e, b_=b_, hq=hq, nxt=nxt: e.copy(out=M_[nxt][:, hq * 4:hq * 4 + 4, :].rearrange("p a b -> p (a b)"), in_=b_[:]), reads=[bk_], writes=[Mk[nxt], bk_])
            for hq in range(2):
                b_, bk_ = bank()
                for hh in range(4):
                    h = hq * 4 + hh
                    st.op("pe", lambda e, b_=b_, hh=hh, h=h, cur=cur, nxt=nxt: e.matmul(b_[:, hh * 128:(hh + 1) * 128], lhsT=A_[nxt][:, h, :], rhs=P_[cur][:, h, :], start=True, stop=True),
                          reads=[Ak[nxt], Pk[cur]], writes=[bk_])
                st.op("dve", lambda e, b_=b_, hq=hq, cur=cur, nxt=nxt: e.tensor_tensor(out=P_[nxt][:, hq * 4:hq * 4 + 4, :].rearrange("p a b -> p (a b)"), in0=b_[:], in1=P_[cur][:, hq * 4:hq * 4 + 4, :].rearrange("p a b -> p (a b)"), op=ALU.add),
                      reads=[bk_, Pk[cur]], writes=[Pk[nxt], bk_])
            cur = nxt
        Pf, Pfk = P_[cur], Pk[cur]
        if STOP == 7:
            return
        cur_pool[0] = "s"
        Zs, Us = X["Zs"], X["Us"]
        bZ, bZk = bank()
        for h in range(8):
            hp, b0 = h // 2, 64 * (h % 2)
            st.op("pe", lambda e, bZ=bZ, h=h, hp=hp, b0=b0: e.matmul(bZ[:, h * 64:(h + 1) * 64], lhsT=ART[b0:b0 + 64, hp, 0, :], rhs=Hb[hp][b0:b0 + 64, :], start=True, stop=False),
                  reads=[K("ART"), ("Hb", hp)], writes=[bZk])
            st.op("pe", lambda e, bZ=bZ, h=h: e.matmul(bZ[:, h * 64:(h + 1) * 64], lhsT=AKT[:, h, 0, :], rhs=vb[:, h * 64:(h + 1) * 64], start=False, stop=True),
                  reads=[K("AKT"), K("vb")], writes=[bZk])
        st.op("act", lambda e, bZ=bZ, Zs=Zs: e.copy(out=Zs[:], in_=bZ[:]), reads=[bZk], writes=[K("Zs"), bZk])
        bU, bUk = bank()
        for h in range(8):
            st.op("pe", lambda e, bU=bU, h=h, Pf=Pf: e.matmul(bU[:, h * 64:(h + 1) * 64], lhsT=Pf[:, h, :], rhs=Zs[:, h * 64:(h + 1) * 64], start=True, stop=True),
                  reads=[Pfk, K("Zs")], writes=[bUk])
        st.op("dve", lambda e, bU=bU, Us=Us: e.tensor_copy(out=Us[:], in_=bU[:]), reads=[bUk], writes=[K("Us"), bUk])
        bY, bYk = bank()
        for h in range(8):
            hp, b0 = h // 2, 64 * (h % 2)
            st.op("pe", lambda e, bY=bY, h=h, hp=hp, b0=b0: e.matmul(bY[:, h * 64:(h + 1) * 64], lhsT=ART[b0:b0 + 64, hp, 1, :], rhs=Hb[hp][b0:b0 + 64, :], start=True, stop=False),
                  reads=[K("ART"), ("Hb", hp)], writes=[bYk])
            st.op("pe", lambda e, bY=bY, h=h: e.matmul(bY[:, h * 64:(h + 1) * 64], lhsT=ABT[:, h, 1, :], rhs=Us[:, h * 64:(h + 1) * 64], start=False, stop=False),
                  reads=[K("ABT"), K("Us")], writes=[bYk])
            st.op("pe", lambda e, bY=bY, h=h: e.matmul(bY[:, h * 64:(h + 1) * 64], lhsT=AKT[:, h, 1, :], rhs=vb[:, h * 64:(h + 1) * 64], start=False, stop=True),
                  reads=[K("AKT"), K("vb")], writes=[bYk])
        bH, bHk = bank()
        for hp in range(4):
            cs = slice(hp * 128, (hp + 1) * 128)
            st.op("pe", lambda e, bH=bH, cs=cs: e.matmul(bH[:, cs], lhsT=Bh[:, cs], rhs=Us[:, cs], start=True, stop=False), reads=[K("Bh"), K("Us")], writes=[bHk])
            st.op("pe", lambda e, bH=bH, cs=cs: e.matmul(bH[:, cs], lhsT=Kh[:, cs], rhs=vb[:, cs], start=False, stop=True), reads=[K("Kh"), K("vb")], writes=[bHk])
        Ysb = X["Ysb"]
        st.op("act", lambda e, bY=bY, Ysb=Ysb: e.copy(out=Ysb[:], in_=bY[:]), reads=[bYk], writes=[K("Ysb"), bYk])
        for hp in range(4):
            for hh in range(2):
                ps_ = slice(hh * 64, (hh + 1) * 64)
                st.op("dve", lambda e, hp=hp, hh=hh, ps_=ps_, bH=bH: e.scalar_tensor_tensor(out=H[hp][ps_, :], in0=H[hp][ps_, :], scalar=gam[ps_, hp:hp + 1], in1=bH[ps_, hp * 128 + hh * 64:hp * 128 + (hh + 1) * 64], op0=ALU.mult, op1=ALU.add),
                      reads=[("H", hp), K("gam"), bHk], writes=[("H", hp), bHk])
            st.op("pool", lambda e, hp=hp: e.tensor_copy(out=Hb[hp][:], in_=H[hp][:]), reads=[("H", hp)], writes=[("Hb", hp)])
        if STOP == 8:
            return
        Ysq, ya = X["Ysq"], X["ya"]
        v3 = lambda t: t[:].rearrange("p (h n) -> p h n", n=64)
        bc = lambda a: a.unsqueeze(2).to_broadcast([128, 8, 64])
        st.op("dve", lambda e, Ysb=Ysb, sm=sm: e.tensor_reduce(out=sm[:, 24:32], in_=v3(Ysb), axis=AX.X, op=ALU.add), reads=[K("Ysb")], writes=[K("sm")])
        st.op("pool", lambda e, Ysb=Ysb, Ysq=Ysq: e.tensor_tensor(out=Ysq[:], in0=Ysb[:], in1=Ysb[:], op=ALU.mult), reads=[K("Ysb")], writes=[K("Ysq")])
        st.op("dve", lambda e, Ysq=Ysq, sm=sm: e.tensor_reduce(out=sm[:, 32:40], in_=v3(Ysq), axis=AX.X, op=ALU.add), reads=[K("Ysq")], writes=[K("sm")])
        st.op("dve", lambda e, sm=sm: e.tensor_scalar_mul(out=sm[:, 40:48], in0=sm[:, 24:32], scalar1=1.0 / 64), reads=[K("sm")], writes=[K("sm")])
        st.op("dve", lambda e, sm=sm: e.tensor_tensor(out=sm[:, 48:56], in0=sm[:, 40:48], in1=sm[:, 40:48], op=ALU.mult), reads=[K("sm")], writes=[K("sm")])
        st.op("dve", lambda e, sm=sm: e.scalar_tensor_tensor(out=sm[:, 48:56], in0=sm[:, 32:40], scalar=1.0 / 64, in1=sm[:, 48:56], op0=ALU.mult, op1=ALU.subtract), reads=[K("sm")], writes=[K("sm")])
        st.op("dve", lambda e, sm=sm: e.tensor_scalar_add(out=sm[:, 48:56], in0=sm[:, 48:56], scalar1=GN_EPS), reads=[K("sm")], writes=[K("sm")])
        st.op("act", lambda e, sm=sm: e.sqrt(out=sm[:, 48:56], in_=sm[:, 48:56]), reads=[K("sm")], writes=[K("sm")])
        st.op("dve", lambda e, sm=sm: e.reciprocal(out=sm[:, 48:56], in_=sm[:, 48:56]), reads=[K("sm")], writes=[K("sm")])
        st.op("dve", lambda e, Ysb=Ysb, ya=ya, sm=sm: e.tensor_tensor(out=v3(ya), in0=v3(Ysb), in1=bc(sm[:, 40:48]), op=ALU.subtract), reads=[K("Ysb"), K("sm")], writes=[K("ya")])
        st.op("dve", lambda e, ya=ya, sm=sm: e.tensor_tensor(out=v3(ya), in0=v3(ya), in1=bc(sm[:, 48:56]), op=ALU.mult), reads=[K("ya"), K("sm")], writes=[K("ya")])
        st.op("pool", lambda e, ya=ya: e.tensor_tensor(out=ya[:], in0=ya[:], in1=lgb[:], op=ALU.mult), reads=[K("ya"), "c"], writes=[K("ya")])
        st.op("pool", lambda e, ya=ya: e.tensor_tensor(out=ya[:], in0=ya[:], in1=lbb[:], op=ALU.add), reads=[K("ya"), "c"], writes=[K("ya")])
        st.op("pool", lambda e, ya=ya, bon=bon: e.tensor_tensor(out=ya[:], in0=ya[:], in1=bon[:], op=ALU.add), reads=[K("ya"), K("bon")], writes=[K("ya")])
        st.op("pool", lambda e, ya=ya, g_=g_: e.tensor_tensor(out=ya[:], in0=ya[:], in1=g_[:], op=ALU.mult), reads=[K("ya"), K("g")], writes=[K("ya")])
        bO, bOk = bank()
        for q in range(4):
            st.op("pe", lambda e, bO=bO, q=q, ya=ya: e.transpose(bO[:, q * 128:(q + 1) * 128], ya[:, q * 128:(q + 1) * 128], identf[:]), reads=[K("ya"), "c"], writes=[bOk])
        q4 = (c // 4) % 2
        c4 = c % 4
        yk = ("yaT4", q4)
        st.op("act", lambda e, bO=bO, q4=q4, c4=c4: e.copy(out=yaT4[q4][:, :, c4 * 128:(c4 + 1) * 128], in_=bO[:].rearrange("p (a b) -> p a b", a=4)), reads=[bOk], writes=[yk, bOk])
        if c4 == 3 or c == nch - 1:
            tb = (c - c4) * 128
            w_ = (c4 + 1) * 128
            st.dma("sync", yT_d[0:512, tb:tb + w_].rearrange("(kc p) t -> p kc t", p=128), yaT4[q4][:, :, 0:w_], reads=[yk], key=("yo", q4))

    for c in range(nch):
        do_chunk(c)
    if finish:
        st.finish()

def rope_tables():
    half = 16
    inv_freq = (np.float32(500000.0) ** (-(np.arange(half, dtype=np.float32)) / np.float32(half))).astype(np.float32)
    ang = (np.arange(T, dtype=np.float32)[:, None] * inv_freq[None, :]).astype(np.float32)
    c = np.cos(ang.astype(np.float64)).astype(np.float32).T
    s = np.sin(ang.astype(np.float64)).astype(np.float32).T
    cos = np.concatenate([c, c], 0)
    sin = np.concatenate([-s, s], 0)
    return np.ascontiguousarray(cos), np.ascontiguousarray(sin)
def pswap():
    P = np.zeros((128, 128), np.float32)
    for m in range(32):
        P[(m + 16) % 32, m] = 1.0
    return P
def dil_mask():
    k = np.arange(128)[:, None]; q = np.arange(128)[None, :]
    A = np.where(k >= q, 0.0, NEG); B = np.where(k <= q, 0.0, NEG)
    return np.ascontiguousarray(np.concatenate([A, B], 1).astype(np.float32))
def moba_cmask():
    k = np.arange(128)[:, None]; q = np.arange(256)[None, :]
    m = np.stack([np.where(128 * c + k <= q, 0.0, NEG) for c in range(2)], 1)
    return np.ascontiguousarray(m.astype(np.float32))
def selall():
    s = np.zeros((16, 16 * 128), np.float32)
    for j in range(16): s[j, j * 128:(j + 1) * 128] = 1.0
    return s
def rwkv_consts():
    s = np.arange(128)[:, None]; t = np.arange(128)[None, :]
    tri = (s <= t).astype(np.float32)
    lt = (s < t).astype(np.float32); le = (s <= t).astype(np.float32)
    mk12 = np.concatenate([lt, le, lt, le], 1)
    gt = (t < s).astype(np.float32)
    mk3 = np.concatenate([gt] * 4, 1)
    return tri, np.ascontiguousarray(mk12), np.ascontiguousarray(mk3)


def _build_program():
    nc = bass.Bass("TRN2", target_bir_lowering=False)
    prog = Prog(nc)

    def din(name, shape, dt=F32):
        return nc.dram_tensor(name, list(shape), dt, kind="ExternalInput").ap()

    def scr(name, shape, dt):
        return nc.dram_tensor(name, list(shape), dt, kind="Internal").ap()

    x = din("x", [T, D])
    ab_w_in = din("ab_w_in", [D, 3360])
    prm = {"shift_mix": din("ab_shift_mix", [1, 1824]), "w0": din("ab_w0", [1, 512]), "w_up": din("ab_w_up", [64, 512]),
           "a0": din("ab_a0", [1, 512]), "a_up": din("ab_a_up", [64, 512]), "g_up": din("ab_g_up", [160, 512]),
           "k_k": din("ab_k_k", [1, 512]), "k_a": din("ab_k_a", [1, 512]), "r_k": din("ab_r_k", [1, 512]),
           "lnx_g": din("ab_lnx_g", [1, 512]), "lnx_b": din("ab_lnx_b", [1, 512])}
    ab_w_out = din("ab_w_out", [D, D])
    c_w_in = din("c_w_in", [D, 5120])
    c_w_out = din("c_w_out", [D, D])
    ln1_g = din("ln1_g", [2, D]); ln1_b = din("ln1_b", [2, D]); ln2_g = din("ln2_g", [2, D]); ln2_b = din("ln2_b", [2, D])
    mlp_w1 = din("mlp_w1", [2, D, 4 * D]); mlp_w2 = din("mlp_w2", [2, 4 * D, D])
    cd = {"ident": din("c_ident", [128, 128]), "ones": din("c_ones", [128, 128]), "cos": din("c_cos", [32, T]), "sin": din("c_sin", [32, T]),
          "pswap": din("c_pswap", [128, 128]), "dmask": din("c_dmask", [128, 256]), "cmask": din("c_cmask", [128, 2, 256]),
          "selall": din("c_selall", [16, 2048]), "tri": din("c_tri", [128, 128]), "mk12": din("c_mk12", [128, 512]), "mk3": din("c_mk3", [128, 512])}
    out = nc.dram_tensor("out", [T, D], F32, kind="ExternalOutput").ap()

    zr = scr("s_zr", [T, 1824], F32)
    mq = scr("s_mq", [4, 128, T], BF16); mk = scr("s_mk", [4, 128, T], BF16); mv = scr("s_mv", [T, 512], BF16)
    yT = scr("s_yT", [D, T], BF16)
    x1 = scr("s_x1", [T, D], F32); x1T = scr("s_x1T", [D, T], BF16)
    xmid = scr("s_xmid", [T, D], F32)
    dq = scr("s_dq", [24, 128, T], BF16); dk = scr("s_dk", [8, 128, T], BF16); dv = scr("s_dv", [8, T, 128], BF16)
    oT = scr("s_oT", [D, T], BF16)

    groups = []
    for h in range(4):
        groups.append({"kind": "fm_rot", "col": 1824 + h * 128, "dst": mq[h]})
        groups.append({"kind": "fm_rot", "col": 1824 + 512 + h * 128, "dst": mk[h]})
    groups.append({"kind": "tm", "col": 1824 + 1024, "n": 512, "dst": mv, "dt": BF16})
    for c0, n in ((0, 512), (512, 512), (1024, 512), (1536, 288)):
        groups.append({"kind": "tm", "col": c0, "n": n, "dst": zr, "dcol": c0, "dt": F32})
    stage_proj(prog, "p0", x, ab_w_in, 0, 3360, groups, cd)
    stage_rwkv(prog, "rw", zr, prm, yT, cd)
    stage_moba(prog, "mb", mq, mk, mv, yT, cd)
    stage_postA(prog, "a0", yT, x, ab_w_out, ln1_g[0:1, :], ln1_b[0:1, :], x1, x1T, cd)
    stage_mlp(prog, "m0", x1, x1T, mlp_w1[0], mlp_w2[0], ln2_g[0:1, :], ln2_b[0:1, :], xmid)
    def _vdst(c):
        return lambda ta, stg: (dv[4 * c:4 * c + 4, ta:ta + 128, :].rearrange("h t c -> t h c"), stg[:, 0:512].rearrange("p (h c) -> p h c", h=4))
    groups = [{"kind": "fm_rot", "col": i * 128, "dst": dq[i]} for i in range(24)]
    groups += [{"kind": "fm_rot", "col": 3072 + i * 128, "dst": dk[i]} for i in range(8)]
    groups += [{"kind": "tm", "col": 4096 + c * 512, "n": 512, "dst": dv, "dst_fn": _vdst(c), "dt": BF16} for c in range(2)]
    stage_proj(prog, "p1", xmid, c_w_in, 0, 5120, groups, cd)
    stage_dil(prog, "dl", dq, dk, dv, oT, cd)
    stage_postA(prog, "a1", oT, xmid, c_w_out, ln1_g[1:2, :], ln1_b[1:2, :], x1, x1T, cd)
    stage_mlp(prog, "m1", x1, x1T, mlp_w1[1], mlp_w2[1], ln2_g[1:2, :], ln2_b[1:2, :], out)
    prog.close()
    return nc


def kernel(**inputs):
    f = lambda a: np.ascontiguousarray(np.asarray(a, dtype=np.float32))
    cos, sin = rope_tables()
    tri, mk12, mk3 = rwkv_consts()
    shared = {
        "ab_w_in": f(inputs["ab_w_in"][0]), "ab_shift_mix": f(inputs["ab_shift_mix"]).reshape(1, 1824), "ab_w0": f(inputs["ab_w0"]).reshape(1, 512),
        "ab_w_up": f(inputs["ab_w_up"][0]), "ab_a0": f(inputs["ab_a0"]).reshape(1, 512), "ab_a_up": f(inputs["ab_a_up"][0]),
        "ab_g_up": f(inputs["ab_g_up"][0]), "ab_k_k": f(inputs["ab_k_k"]).reshape(1, 512), "ab_k_a": f(inputs["ab_k_a"]).reshape(1, 512),
        "ab_r_k": f(inputs["ab_r_k"]).reshape(1, 512), "ab_lnx_g": f(inputs["ab_lnx_g"]).reshape(1, 512), "ab_lnx_b": f(inputs["ab_lnx_b"]).reshape(1, 512),
        "ab_w_out": f(inputs["ab_w_out"][0]), "c_w_in": f(inputs["c_w_in"][0]), "c_w_out": f(inputs["c_w_out"][0]),
        "ln1_g": f(inputs["ln1_g"]), "ln1_b": f(inputs["ln1_b"]), "ln2_g": f(inputs["ln2_g"]), "ln2_b": f(inputs["ln2_b"]),
        "mlp_w1": f(inputs["mlp_w1"]), "mlp_w2": f(inputs["mlp_w2"]),
        "c_ident": np.eye(128, dtype=np.float32), "c_ones": np.ones((128, 128), np.float32), "c_cos": cos, "c_sin": sin, "c_pswap": pswap(),
        "c_dmask": dil_mask(), "c_cmask": moba_cmask(), "c_selall": selall(), "c_tri": tri, "c_mk12": mk12, "c_mk3": mk3,
    }
    xs = f(inputs["x"])
    nc = _build_program()
    in_maps = [dict(shared, x=np.ascontiguousarray(xs[b])) for b in range(8)]
    res = run_bass_kernel_spmd(nc, in_maps, core_ids=list(range(8)))
    return np.stack([np.asarray(r["out"], dtype=np.float32) for r in res.results], 0)
```
